# Optimizing a Trainium2 kernel written in Bass

```python
import jax, jax.numpy as jnp
from jax import lax
import numpy as np

D_MODEL = 2048
BATCH = 2
SEQ = 8192
DEPTH = 1
DEC_BATCH = 128
DEC_SEQ = 8
PAST_LEN = 16384
PAGE_SIZE = 128

HEAD_DIM = 64
ROT_DIM = HEAD_DIM // 4
ROPE_THETA = 500000.0
BLK = 128
A_WINDOW = 128
A_Q_HEADS = 16
A_KV_HEADS = 2
A_GROUP = A_Q_HEADS // A_KV_HEADS
B_PAIRS = ((128, 1), (512, 4), (2048, 16))
B_Q_HEADS = 8
B_KV_HEADS = 4
B_GROUP = B_Q_HEADS // B_KV_HEADS
A_WIDTH = A_Q_HEADS * HEAD_DIM
B_WIDTH = B_Q_HEADS * HEAD_DIM
A_KV_WIDTH = A_KV_HEADS * HEAD_DIM
B_KV_WIDTH = B_KV_HEADS * HEAD_DIM
SIZES = ([A_WIDTH, A_KV_WIDTH, A_KV_WIDTH, A_WIDTH]
         + [B_WIDTH, B_KV_WIDTH, B_KV_WIDTH] * len(B_PAIRS)
         + [B_WIDTH, D_MODEL, D_MODEL])
IN_WIDTH = sum(SIZES)
SPLIT_POINTS = tuple(int(c) for c in np.cumsum(SIZES)[:-1])
SCALE = HEAD_DIM ** -0.5
DN_ALPHA = (2 * DEPTH) ** 0.25
DN_BETA = (8 * DEPTH) ** -0.25
LN_EPS = 1e-5

kernel_name = 'hybrid_gated_swa_dilated_step'


def _rope(x, pos):
    half = ROT_DIM // 2
    inv = ROPE_THETA ** (-jnp.arange(0, ROT_DIM, 2, dtype=jnp.float32) / ROT_DIM)
    ang = pos[:, None] * inv[None, :]
    shape = (1, pos.shape[0]) + (1,) * (x.ndim - 3) + (half,)
    cos = jnp.cos(ang).reshape(shape)
    sin = jnp.sin(ang).reshape(shape)
    xr = x[..., :ROT_DIM].astype(jnp.float32)
    x1, x2 = xr[..., :half], xr[..., half:]
    rot = jnp.concatenate([x1 * cos - x2 * sin, x2 * cos + x1 * sin], axis=-1).astype(x.dtype)
    return jnp.concatenate([rot, x[..., ROT_DIM:]], axis=-1)


def _masked_softmax(s, valid, sink):
    s = jnp.where(valid, s, -jnp.inf)
    m = jnp.max(s, axis=-1, keepdims=True)
    if sink is not None:
        m = jnp.maximum(m, sink)
    p = jnp.exp(s - m)
    den = jnp.sum(p, axis=-1, keepdims=True)
    if sink is not None:
        den = den + jnp.exp(sink - m)
    return p / den, (m + jnp.log(den))[..., 0]


def _banded_attention(q, k, v, max_dist, sink):
    n, l, hk, g, d = q.shape
    nb = -(-l // BLK)
    pad = nb * BLK - l
    q = jnp.pad(q, ((0, 0), (0, pad), (0, 0), (0, 0), (0, 0)))
    k = jnp.pad(k, ((0, 0), (BLK, pad), (0, 0), (0, 0)))
    v = jnp.pad(v, ((0, 0), (BLK, pad), (0, 0), (0, 0)))
    qb = q.reshape(n, nb, BLK, hk, g, d)
    kb = k.reshape(n, nb + 1, BLK, hk, d)
    vb = v.reshape(n, nb + 1, BLK, hk, d)
    kw = jnp.concatenate([kb[:, :-1], kb[:, 1:]], axis=2)
    vw = jnp.concatenate([vb[:, :-1], vb[:, 1:]], axis=2)
    s = jnp.einsum('nbqhgd,nbkhd->nbhgqk', qb, kw).astype(jnp.float32) * SCALE
    qi = np.arange(BLK)[:, None]
    kj = np.arange(2 * BLK)[None, :]
    dist = BLK + qi - kj
    keypos = (np.arange(nb)[:, None, None] - 1) * BLK + kj[None]
    valid = (dist >= 0) & (dist <= max_dist) & (keypos >= 0)
    p, lse = _masked_softmax(s, jnp.asarray(valid)[None, :, None, None], sink)
    o = jnp.einsum('nbhgqk,nbkhd->nbqhgd', p.astype(v.dtype), vw)
    o = o.reshape(n, nb * BLK, hk, g, d)[:, :l]
    lse = jnp.transpose(lse, (0, 1, 4, 2, 3)).reshape(n, nb * BLK, hk, g)[:, :l]
    return o, lse


def _gathered_attention(q, k, v, start, q_pos, dists, sink):
    kpos = q_pos[:, None] - dists[None, :]
    valid = jnp.asarray(kpos >= 0)
    idx = jnp.asarray(np.clip(kpos - start, 0, k.shape[1] - 1))
    kg = k[:, idx]
    vg = v[:, idx]
    s = jnp.einsum('nthgd,ntjhd->nthgj', q, kg).astype(jnp.float32) * SCALE
    p, lse = _masked_softmax(s, valid[None, :, None, None, :], sink)
    o = jnp.einsum('nthgj,ntjhd->nthgd', p.astype(v.dtype), vg)
    return o, lse


def _fold(x, d):
    n, l = x.shape[:2]
    rest = x.shape[2:]
    x = x.reshape((n, l // d, d) + rest)
    return jnp.swapaxes(x, 1, 2).reshape((n * d, l // d) + rest)


def _unfold(x, n, d):
    m = x.shape[1]
    rest = x.shape[2:]
    x = x.reshape((n, d, m) + rest)
    return jnp.swapaxes(x, 1, 2).reshape((n, m * d) + rest)


def _combine_dilations(outs, lses):
    w = jax.nn.softmax(jnp.stack(lses, 0), axis=0)
    o = jnp.stack(outs, 0).astype(jnp.float32)
    return jnp.sum(w[..., None] * o, axis=0).astype(outs[0].dtype)


def _in_proj(x, w_in, b_in):
    n, l, _ = x.shape
    parts = jnp.split(x @ w_in + b_in, SPLIT_POINTS, axis=-1)
    qa = parts[0].reshape(n, l, A_KV_HEADS, A_GROUP, HEAD_DIM)
    ka = parts[1].reshape(n, l, A_KV_HEADS, HEAD_DIM)
    va = parts[2].reshape(n, l, A_KV_HEADS, HEAD_DIM)
    za = parts[3]
    qb = [parts[4 + 3 * i].reshape(n, l, B_KV_HEADS, B_GROUP, HEAD_DIM) for i in range(len(B_PAIRS))]
    kb = [parts[5 + 3 * i].reshape(n, l, B_KV_HEADS, HEAD_DIM) for i in range(len(B_PAIRS))]
    vb = [parts[6 + 3 * i].reshape(n, l, B_KV_HEADS, HEAD_DIM) for i in range(len(B_PAIRS))]
    zb, ga, gb = parts[-3], parts[-2], parts[-1]
    return qa, ka, va, za, qb, kb, vb, zb, ga, gb


def _out_proj(x, oa, ob, za, zb, ga, gb, w_br_a, w_br_b, w_out, ln_g, ln_b):
    n, l, _ = x.shape
    ya = oa.reshape(n, l, A_WIDTH) * jax.nn.silu(za)
    yb = ob.reshape(n, l, B_WIDTH) * jax.nn.silu(zb)
    m = jax.nn.sigmoid(ga) * (ya @ w_br_a) + jax.nn.sigmoid(gb) * (yb @ w_br_b)
    h = (DN_ALPHA * x + m @ w_out).astype(jnp.float32)
    mu = jnp.mean(h, axis=-1, keepdims=True)
    var = jnp.mean(jnp.square(h - mu), axis=-1, keepdims=True)
    return ((h - mu) * lax.rsqrt(var + LN_EPS) * ln_g + ln_b).astype(x.dtype)


def _prompt_layer(x, w_in, b_in, sink_a, w_br_a, w_br_b, w_out, ln_g, ln_b):
    n, l, _ = x.shape
    pos = jnp.arange(l, dtype=jnp.float32)
    qa, ka, va, za, qb, kb, vb, zb, ga, gb = _in_proj(x, w_in, b_in)
    qa, ka = _rope(qa, pos), _rope(ka, pos)
    oa, _ = _banded_attention(qa, ka, va, A_WINDOW - 1, sink_a.reshape(A_KV_HEADS, A_GROUP, 1, 1))
    wa = min(A_WINDOW, l)
    states = [jnp.stack([ka, va], axis=2)[:, l - wa:]]
    outs, lses = [], []
    for (win, dil), q, k, v in zip(B_PAIRS, qb, kb, vb):
        q, k = _rope(q, pos), _rope(k, pos)
        o, lse = _banded_attention(_fold(q, dil), _fold(k, dil), _fold(v, dil), win // dil, None)
        outs.append(_unfold(o, n, dil))
        lses.append(_unfold(lse, n, dil))
        wb = min(win, l)
        states.append(jnp.stack([k, v], axis=2)[:, l - wb:])
    ob = _combine_dilations(outs, lses)
    y = _out_proj(x, oa, ob, za, zb, ga, gb, w_br_a, w_br_b, w_out, ln_g, ln_b)
    return y, states


def _sample_layer(x, cache_a, caches_b, w_in, b_in, sink_a, w_br_a, w_br_b, w_out, ln_g, ln_b):
    n, t, _ = x.shape
    pos = PAST_LEN + jnp.arange(t, dtype=jnp.float32)
    q_pos = PAST_LEN + np.arange(t)
    qa, ka, va, za, qb, kb, vb, zb, ga, gb = _in_proj(x, w_in, b_in)
    qa, ka = _rope(qa, pos), _rope(ka, pos)
    ca = cache_a.shape[1]
    kv_a = jnp.concatenate([cache_a, jnp.stack([ka, va], axis=2)], axis=1)
    oa, _ = _gathered_attention(qa, kv_a[:, :, 0], kv_a[:, :, 1], PAST_LEN - ca, q_pos,
                                np.arange(A_WINDOW), sink_a.reshape(A_KV_HEADS, A_GROUP, 1))
    states = [kv_a[:, -ca:]]
    outs, lses = [], []
    for (win, dil), cache, q, k, v in zip(B_PAIRS, caches_b, qb, kb, vb):
        q, k = _rope(q, pos), _rope(k, pos)
        cb = cache.shape[1]
        kv = jnp.concatenate([cache, jnp.stack([k, v], axis=2)], axis=1)
        o, lse = _gathered_attention(q, kv[:, :, 0], kv[:, :, 1], PAST_LEN - cb, q_pos,
                                     np.arange(0, win + 1, dil), None)
        outs.append(o)
        lses.append(lse)
        states.append(kv[:, -cb:])
    ob = _combine_dilations(outs, lses)
    y = _out_proj(x, oa, ob, za, zb, ga, gb, w_br_a, w_br_b, w_out, ln_g, ln_b)
    return y, states


def setup_inputs(seed: int = 0) -> dict:
    key = jax.random.key(seed)
    ks = jax.random.split(key, 16)
    f32 = jnp.float32
    nrm = lambda k, shape: jax.random.normal(k, shape, f32)
    la = min(A_WINDOW, PAST_LEN)
    lb = [min(w, PAST_LEN) for w, _ in B_PAIRS]
    return {
        'x_prompt': nrm(ks[0], (BATCH, SEQ, D_MODEL)),
        'x_sample': nrm(ks[1], (DEC_BATCH, DEC_SEQ, D_MODEL)),
        'cache_a_kv': nrm(ks[2], (DEPTH, DEC_BATCH, la, 2, A_KV_HEADS, HEAD_DIM)),
        'cache_b0_kv': nrm(ks[3], (DEPTH, DEC_BATCH, lb[0], 2, B_KV_HEADS, HEAD_DIM)),
        'cache_b1_kv': nrm(ks[4], (DEPTH, DEC_BATCH, lb[1], 2, B_KV_HEADS, HEAD_DIM)),
        'cache_b2_kv': nrm(ks[5], (DEPTH, DEC_BATCH, lb[2], 2, B_KV_HEADS, HEAD_DIM)),
        'w_in': nrm(ks[6], (DEPTH, D_MODEL, IN_WIDTH)) * D_MODEL ** -0.5,
        'b_in': nrm(ks[7], (DEPTH, IN_WIDTH)) * 0.02,
        'sink_a': nrm(ks[8], (DEPTH, A_Q_HEADS)) * 0.5,
        'w_br_a': nrm(ks[9], (DEPTH, A_WIDTH, D_MODEL)) * (A_WIDTH ** -0.5 * DN_BETA),
        'w_br_b': nrm(ks[10], (DEPTH, B_WIDTH, D_MODEL)) * (B_WIDTH ** -0.5 * DN_BETA),
        'w_out': nrm(ks[11], (DEPTH, D_MODEL, D_MODEL)) * (D_MODEL ** -0.5 * DN_BETA),
        'ln_g': 1.0 + 0.02 * nrm(ks[12], (DEPTH, D_MODEL)),
        'ln_b': 0.02 * nrm(ks[13], (DEPTH, D_MODEL)),
    }


def reference(x_prompt, x_sample, cache_a_kv, cache_b0_kv, cache_b1_kv, cache_b2_kv,
              w_in, b_in, sink_a, w_br_a, w_br_b, w_out, ln_g, ln_b):
    hp, hs = x_prompt, x_sample
    prompt_st = [[], [], [], []]
    sample_st = [[], [], [], []]
    for layer in range(DEPTH):
        hp, sp = _prompt_layer(hp, w_in[layer], b_in[layer], sink_a[layer], w_br_a[layer],
                               w_br_b[layer], w_out[layer], ln_g[layer], ln_b[layer])
        hs, ss = _sample_layer(hs, cache_a_kv[layer],
                               (cache_b0_kv[layer], cache_b1_kv[layer], cache_b2_kv[layer]),
                               w_in[layer], b_in[layer], sink_a[layer], w_br_a[layer],
                               w_br_b[layer], w_out[layer], ln_g[layer], ln_b[layer])
        for lst, s in zip(prompt_st, sp):
            lst.append(s)
        for lst, s in zip(sample_st, ss):
            lst.append(s)
    prompt_a_kv, prompt_b0_kv, prompt_b1_kv, prompt_b2_kv = [jnp.stack(s, 0) for s in prompt_st]
    sample_a_kv, sample_b0_kv, sample_b1_kv, sample_b2_kv = [jnp.stack(s, 0) for s in sample_st]
    return (hp, hs, prompt_a_kv, prompt_b0_kv, prompt_b1_kv, prompt_b2_kv,
            sample_a_kv, sample_b0_kv, sample_b1_kv, sample_b2_kv)
```

```python
import numpy as np
from contextlib import ExitStack
import concourse.bass as bass
import concourse.mybir as mybir
from concourse.bass_utils import run_bass_kernel_spmd

F32 = mybir.dt.float32
BF16 = mybir.dt.bfloat16
AF = mybir.ActivationFunctionType
ALU = mybir.AluOpType

NCORE = 8
DM = 2048
CH = 2048
NSAMP = 16
PAST = 16384
NTOK = CH + 128
ALPHA = 2.0 ** 0.25
LN_EPS = 1e-5
SCALE = 0.125
THETA = 500000.0

GROUPS = [
    dict(name="B2", col0=4352, nq=512, nk=256, Hkv=4, G=2, dil=16, C=2048, order="f16", pst_rows=2048),
    dict(name="B1", col0=3328, nq=512, nk=256, Hkv=4, G=2, dil=4, C=512, order="f4", pst_rows=512),
    dict(name="B0", col0=2304, nq=512, nk=256, Hkv=4, G=2, dil=1, C=128, order="nat", pst_rows=128),
    dict(name="A", col0=0, nq=1024, nk=128, Hkv=2, G=8, dil=1, C=128, order="nat", pst_rows=128),
]
ZA0, ZB0, GA0, GB0 = 1280, 5376, 5888, 7936

DBG = {"groups": None, "tiles": None, "out": True, "bgcopy": True, "samp": True, "attn": True}

M_CUR, M_PA_H, M_PB_H, M_PA, M_PB, M_S1, M_S4, M_S16 = 0, 128, 256, 384, 512, 640, 768, 896
M_CA, M_CB0, M_CB1, M_ID = 1024, 1032, 1040, 1056
NMASK = 1184


def sl(start, n, step):
    return slice(start, start + (n - 1) * step + 1, step)


def tile_list(order):
    tl = []
    if order == "nat":
        tl.append(dict(kind="halo", row0=2048 - 128, step=1, slot=0))
        for b in range(16):
            tl.append(dict(kind="main", row0=2048 + 128 * b, step=1, slot=1 + b % 3,
                           prev=(0 if b == 0 else 1 + (b - 1) % 3), prev_halo=(b == 0), col0=128 * b, cstep=1))
    elif order == "f4":
        for r in range(4):
            tl.append(dict(kind="halo", row0=2048 - 512 + r, step=4, slot=r))
        for t in range(16):
            M, r = divmod(t, 4)
            tl.append(dict(kind="main", row0=2048 + 512 * M + r, step=4, slot=4 + t % 8,
                           prev=(r if M == 0 else 4 + (t - 4) % 8), prev_halo=(M == 0), col0=512 * M + r, cstep=4))
    else:
        for r in range(16):
            tl.append(dict(kind="halo", row0=r, step=16, slot=r))
        for r in range(16):
            tl.append(dict(kind="main", row0=2048 + r, step=16, slot=16 + r % 2, prev=r, prev_halo=True, col0=r, cstep=16))
    tl.append(dict(kind="samp", row0=4096, step=1, slot={"nat": 2, "f4": 4, "f16": 16}[order], col0=2048, cstep=1))
    return tl


class Buf:
    __slots__ = ("last_w", "readers")

    def __init__(self):
        self.last_w = None
        self.readers = {}


class _Rec:
    def __getattr__(self, name):
        def f(*a, **kw):
            return (name, a, kw)
        return f


_REC = _Rec()


class Sched:
    ENGS = ("pe", "act", "dve", "pool", "sp")

    def __init__(self, nc, stack, n_dma_sems=90):
        self.nc = nc
        self.sems = {}
        for k in self.ENGS:
            self.sems[k] = stack.enter_context(nc.semaphore(f"sem_{k}"))
        self.n_dma = n_dma_sems
        for j in range(n_dma_sems):
            self.sems[f"d{j}"] = stack.enter_context(nc.semaphore(f"sem_d{j}"))
        self.sems["bg"] = stack.enter_context(nc.semaphore("sem_bg"))
        self.bg_val = 0
        self.dma_val = [0] * n_dma_sems
        self.dma_pool = {"sp": list(range(0, 44)), "pool": list(range(44, 86)), "act": list(range(86, n_dma_sems))}
        self.dma_rr = {"sp": 0, "pool": 0, "act": 0}
        self.cnt = {k: 0 for k in self.ENGS}
        self.prog = {k: [] for k in self.ENGS}
        self.waited = {k: {} for k in self.ENGS}

    def _deps(self, reads, writes):
        deps = {}

        def add(k, v):
            if deps.get(k, 0) < v:
                deps[k] = v
        for b in reads:
            if b.last_w is not None:
                add(*b.last_w)
        for b in writes:
            if b.last_w is not None:
                add(*b.last_w)
            for k, v in b.readers.items():
                add(k, v)
        return deps

    def _commit(self, tok, reads, writes):
        k, v = tok
        for b in reads:
            if b.readers.get(k, 0) < v:
                b.readers[k] = v
        for b in writes:
            b.last_w = tok
            b.readers = {}

    def _waits(self, eng, deps, skip_self_pe=True):
        waits = []
        w = self.waited[eng]
        for k, v in deps.items():
            if k == eng and eng == "pe" and skip_self_pe:
                continue
            if w.get(k, 0) >= v:
                continue
            w[k] = v
            waits.append((k, v))
        return waits

    def op(self, eng, fn, reads=(), writes=()):
        deps = self._deps(reads, writes)
        waits = self._waits(eng, deps)
        self.cnt[eng] += 1
        tok = (eng, self.cnt[eng])
        self.prog[eng].append((waits, fn(_REC), tok, 1))
        self._commit(tok, reads, writes)
        return tok

    def dma_bg(self, fn, eng="act"):
        self.bg_val += 16
        self.prog[eng].append(([], fn(_REC), ("bg", self.bg_val), 16))

    def dma(self, fn, reads=(), writes=(), eng="sp"):
        deps = self._deps(reads, writes)
        pl = self.dma_pool[eng]
        j = pl[self.dma_rr[eng] % len(pl)]
        self.dma_rr[eng] += 1
        key = f"d{j}"
        if self.dma_val[j] > 0 and deps.get(key, 0) < self.dma_val[j]:
            deps[key] = self.dma_val[j]
        waits = self._waits(eng, deps, skip_self_pe=False)
        self.dma_val[j] += 16
        tok = (key, self.dma_val[j])
        self.prog[eng].append((waits, fn(_REC), tok, 16))
        self._commit(tok, reads, writes)
        return tok

    def barrier(self):
        allw = [(k, self.cnt[k]) for k in self.ENGS if self.cnt[k] > 0]
        allw += [(f"d{j}", self.dma_val[j]) for j in range(self.n_dma) if self.dma_val[j] > 0]
        if self.bg_val > 0:
            allw.append(("bg", self.bg_val))
        for eng in self.ENGS:
            waits = []
            w = self.waited[eng]
            for k, v in allw:
                if k == eng:
                    continue
                if w.get(k, 0) >= v:
                    continue
                w[k] = v
                waits.append((k, v))
            self.prog[eng].append((waits, None, None, 0))

    def emit(self):
        nc, sems, prog = self.nc, self.sems, self.prog

        def run(e, items):
            for waits, fn, tok, inc in items:
                for k, v in waits:
                    e.wait_ge(sems[k], v)
                if fn is None:
                    continue
                ins = getattr(e, fn[0])(*fn[1], **fn[2])
                ins.then_inc(sems[tok[0]], inc)

        with nc.Block() as block:
            @block.sync
            def _(e):
                run(e, prog["sp"])

            @block.tensor
            def _(e):
                run(e, prog["pe"])

            @block.scalar
            def _(e):
                run(e, prog["act"])

            @block.vector
            def _(e):
                run(e, prog["dve"])

            @block.gpsimd
            def _(e):
                run(e, prog["pool"])


def build_program():
    nc = bass.Bass("TRN2", target_bir_lowering=False)

    def din(name, shape, dt=F32):
        return nc.dram_tensor(name, shape, dt, kind="ExternalInput").ap()

    def dout(name, shape, dt=F32):
        return nc.dram_tensor(name, shape, dt, kind="ExternalOutput").ap()

    xe = din("xe", [4224, DM])
    w_in = din("w_in", [DM, 9984])
    b_in = din("b_in", [1, 9984])
    bfm_d = din("bfm", [128, 44])
    bzrow_d = din("bzrow", [1, 1536])
    sink_d = din("sinkl", [128, 8])
    w_br_a = din("w_br_a", [1024, DM])
    w_br_b = din("w_br_b", [512, DM])
    w_out = din("w_out", [DM, DM])
    ln_g = din("ln_g", [1, DM])
    ln_b = din("ln_b", [1, DM])
    cs_d = din("cs", [4, 128, 33, 16])
    masks_d = din("masks", [128, NMASK])
    cache = {"A": din("cache_a", [NSAMP, 128, 256]), "B0": din("cache_b0", [NSAMP, 128, 512]),
             "B1": din("cache_b1", [NSAMP, 512, 512]), "B2": din("cache_b2", [NSAMP, 2048, 512])}
    y_d = dout("y", [NTOK, DM])
    pst = {"A": dout("pst_a", [128, 256]), "B0": dout("pst_b0", [128, 512]),
           "B1": dout("pst_b1", [512, 512]), "B2": dout("pst_b2", [2048, 512])}
    sst = {"A": dout("sst_a", [NSAMP, 128, 256]), "B0": dout("sst_b0", [NSAMP, 128, 512]),
           "B1": dout("sst_b1", [NSAMP, 512, 512]), "B2": dout("sst_b2", [NSAMP, 2048, 512])}
    ya_s = nc.dram_tensor("ya_s", [128, 8, NTOK], BF16, kind="Internal").ap()
    yb_s = nc.dram_tensor("yb_s", [128, 4, NTOK], BF16, kind="Internal").ap()

    with ExitStack() as st:
        S = Sched(nc, st)

        uniq = [0]

        def sb(stack, name, shp, dt):
            uniq[0] += 1
            return stack.enter_context(nc.sbuf_tensor(f"{name}_{uniq[0]}", shp, dt))

        maskt = sb(st, "maskt", [128, NMASK], BF16); B_mask = Buf()
        ones = sb(st, "ones", [128, 128], BF16); B_ones = Buf()
        bzr = sb(st, "bzr", [128, 1536], BF16); B_bzr = Buf()
        bfm = sb(st, "bfm_sb", [128, 44], F32); B_bfm = Buf()
        xb = [sb(st, f"xb{i}", [128, DM], BF16) for i in range(3)]; B_xb = [Buf(), Buf(), Buf()]
        ident = maskt[:, M_ID:M_ID + 128]

        S.dma(lambda e: e.dma_start(out=maskt[:], in_=masks_d), writes=[B_mask], eng="pool")
        S.dma(lambda e: e.dma_start(out=bfm[:], in_=bfm_d), writes=[B_bfm])
        S.op("pool", lambda e: e.memset(bzr[:], 0.0), writes=[B_bzr])
        S.dma(lambda e: e.dma_start(out=bzr[0:1, :], in_=bzrow_d), writes=[B_bzr], eng="pool")
        S.op("dve", lambda e: e.memset(ones[:], 1.0), writes=[B_ones])

        bg_pending = []
        for g in (GROUPS if DBG["bgcopy"] else []):
            nm, C = g["name"], g["C"]
            for s in range(NSAMP):
                bg_pending.append((nm, C, s))

        def issue_bg(n=1):
            for _ in range(n):
                if bg_pending:
                    nm, C, s = bg_pending.pop(0)
                    S.dma_bg(lambda e: e.dma_start(out=sst[nm][s, 0:C - 8, :], in_=cache[nm][s, 8:C, :]))

        WCH = 2 * (2048 + 2048 + 1024 + 512)
        wsc = nc.dram_tensor("wsc", [8, 128, WCH], BF16, kind="Internal").ap()
        B_wsc = [Buf() for _ in range(8)]
        wosc = nc.dram_tensor("wosc", [128, 16, DM], BF16, kind="Internal").ap()
        B_wosc = [Buf() for _ in range(4)]
        wconv_pending = []
        for cg2 in range(8):
            c0_ = 256 * cg2
            dst = wsc[cg2]
            wconv_pending.append((cg2, dst[:, 0:4096].rearrange("p (k n) -> p k n", k=16),
                                  w_in[:, GA0 + c0_:GA0 + c0_ + 256].rearrange("(k p) n -> p k n", p=128)))
            wconv_pending.append((cg2, dst[:, 4096:8192].rearrange("p (k n) -> p k n", k=16),
                                  w_in[:, GB0 + c0_:GB0 + c0_ + 256].rearrange("(k p) n -> p k n", p=128)))
            for e2 in range(2):
                wconv_pending.append((cg2, dst[64 * e2:64 * e2 + 64, 8192:10240].rearrange("p (k n) -> p k n", k=8),
                                      w_br_a[e2 * 512:(e2 + 1) * 512, c0_:c0_ + 256].rearrange("(g d) n -> d g n", d=64)))
                for cp in range(2):
                    wconv_pending.append((cg2, dst[64 * e2:64 * e2 + 64, 10240:11264].rearrange("p (k n) -> p k n", k=4)[:, cp * 2:cp * 2 + 2, :],
                                          w_br_b[(2 * cp + e2) * 128:(2 * cp + e2 + 1) * 128, c0_:c0_ + 256]
                                          .rearrange("(g d) n -> d g n", d=64)))

        for cg in range(4):
            wconv_pending.append((("o", cg), wosc[:, :, cg * 512:(cg + 1) * 512],
                                  w_out[:, cg * 512:(cg + 1) * 512].rearrange("(k p) n -> p k n", p=128)))

        def issue_wconv(n):
            for _ in range(n):
                if wconv_pending:
                    cg2, o_, i_ = wconv_pending.pop(0)
                    Bw_ = B_wosc[cg2[1]] if isinstance(cg2, tuple) else B_wsc[cg2]
                    S.dma(lambda e: e.dma_start(out=o_, in_=i_), writes=[Bw_], eng="pool")

        evac_flip = [0]

        def evac_copy(out_ap, in_ap, reads, writes):
            evac_flip[0] ^= 1
            if evac_flip[0]:
                S.op("act", lambda e: e.activation(out=out_ap, in_=in_ap, func=AF.Copy), reads=reads, writes=writes)
            else:
                S.op("dve", lambda e: e.tensor_copy(out=out_ap, in_=in_ap), reads=reads, writes=writes)

        class RR:
            def __init__(self, items):
                self.items = items
                self.i = 0

            def next(self):
                it = self.items[self.i % len(self.items)]
                self.i += 1
                return it

        def x_load(row0, step, i):
            S.dma(lambda e: e.dma_start(out=xb[i][:], in_=xe[sl(row0, 128, step), :]), writes=[B_xb[i]], eng="pool")

        def pe_transposes(srcs, reads, trr, dst_fn, dst_buf):
            for j0 in range(0, len(srcs), 4):
                grp = srcs[j0:j0 + 4]
                ptr, Bp = trr.next()
                for j, src in enumerate(grp):
                    S.op("pe", lambda e: e.matmul(ptr[:, j * 128:(j + 1) * 128], lhsT=src, rhs=ident, start=True, stop=True),
                         reads=reads + [B_mask], writes=[Bp])
                evac_copy(dst_fn(j0, len(grp)), ptr[:, 0:len(grp) * 128].rearrange("p (a b) -> p a b", a=len(grp)), [Bp], [dst_buf])
                yield

        def x_transpose(i, dst, dst_buf, trr):
            yield from pe_transposes([xb[i][:, k * 128:(k + 1) * 128] for k in range(16)], [B_xb[i]], trr,
                                     lambda j0, n: dst[:, j0:j0 + n, :], dst_buf)

        def run_interleaved(gens):
            alive = [g for g in gens if g is not None]
            while alive:
                for g in list(alive):
                    try:
                        next(g)
                    except StopIteration:
                        alive.remove(g)

        def attention_phase(groups, ph, phaseA):
            def psum(name, shp, dt=F32):
                uniq[0] += 1
                return ph.enter_context(nc.psum_tensor(f"{name}_{uniq[0]}", shp, dt)), Buf()
            p_st, B_pst = psum("p_st", [128, 1024])
            if phaseA:
                p_ot, B_pot = psum("p_ot", [128, 1024])
                p_den, B_pden = psum("p_den", [128, 1024])
                acc_rr = RR([psum("p_acc0", [128, 512])])
                tr_rr = RR([psum("p_tr0", [128, 512])])
            else:
                p_ot, B_pot = psum("p_ot", [128, 512])
                p_den, B_pden = psum("p_den", [128, 512])
                acc_rr = RR([psum("p_acc0", [128, 512]), psum("p_acc1", [128, 512])])
                tr_rr = RR([psum("p_tr0", [128, 512]), psum("p_tr1", [128, 512])])
            WQC = 1280 if phaseA else 1024
            WZC = 1024 if phaseA else 512
            wq = sb(ph, "wq", [128, 16, WQC], BF16); B_wq = Buf()
            wz = sb(ph, "wz", [128, 16, WZC], BF16); B_wz = Buf()
            bias_bc = sb(ph, "bias_bc", [128, WQC], F32); B_bias = Buf()
            cs_sb = sb(ph, "cs_sb", [128, 33, 16], F32); B_cs = Buf()
            kT_all = sb(ph, "kT_all", [128, 18, 2, 128], BF16); B_kT = [Buf() for _ in range(18)]
            v_all = sb(ph, "v_all", [128, 18, 256], BF16); B_v = [Buf() for _ in range(18)]
            xT = [sb(ph, f"xT{i}", [128, 16, 128], BF16) for i in range(2)]; B_xT = [Buf(), Buf()]
            qkv = [sb(ph, f"qkv{i}", [128, WQC], F32) for i in range(3)]; B_qkv = [Buf(), Buf(), Buf()]
            rt = [sb(ph, f"rt{i}", [128, 18, 8], F32) for i in range(4)]; B_rts = [Buf() for _ in range(4)]
            qbs = [sb(ph, f"qb{i}", [128, 1024], BF16) for i in range(2)]; B_qbs = [Buf(), Buf()]
            kbs = [sb(ph, f"kb{i}", [128, 256], BF16) for i in range(2)]; B_kbs = [Buf(), Buf()]
            qT = [sb(ph, f"qT{i}", [128, 8, 128], BF16) for i in range(2)]; B_qT = [Buf(), Buf()]
            PTs = [sb(ph, f"PT{i}", [128, 1024], BF16) for i in range(2)]; B_PTs = [Buf(), Buf()]
            pt_rr = [0]
            dsums = [sb(ph, f"dsum{i}", [128, 1024 if phaseA else 512], F32) for i in range(2)]; B_dsums = [Buf(), Buf()]
            nsums = [sb(ph, f"nsum{i}", [128, 1024 if phaseA else 512], F32) for i in range(2)]; B_nsums = [Buf(), Buf()]
            zTs = [sb(ph, f"zT{i}", [128, 8 if phaseA else 4, 128], BF16) for i in range(3)]; B_zTs = [Buf(), Buf(), Buf()]
            ytiles = [sb(ph, f"ytile{i}", [128, 8, 128], BF16) for i in range(2)]; B_yts = [Buf(), Buf()]
            cbks = [sb(ph, f"cbk{i}", [128, 8, 512], BF16) for i in range(2)]; B_cbks = [Buf(), Buf()]
            kTc = sb(ph, "kTc", [128, 8, 2, 128], BF16); B_kTc = Buf()
            if phaseA:
                sinkexp = sb(ph, "sinkexp", [128, 8], F32); B_sink = Buf()
                S.dma(lambda e: e.dma_start(out=sinkexp[:], in_=sink_d), writes=[B_sink])
                S.op("act", lambda e: e.activation(out=sinkexp[:], in_=sinkexp[:], func=AF.Exp), reads=[B_sink], writes=[B_sink])
            else:
                accN = sb(ph, "accN", [128, 4, NTOK], BF16); B_accN = Buf()
                accD = sb(ph, "accD", [128, 4, NTOK], BF16); B_accD = Buf()

            for gi, g in enumerate(groups):
                nm, col0, nq, nk, Hkv, G, dil, C = (g[k] for k in ("name", "col0", "nq", "nk", "Hkv", "G", "dil", "C"))
                isA = nm == "A"
                pidx = [x["name"] for x in GROUPS].index(nm)
                ncol = nq + 2 * nk
                NCH = Hkv // 2
                NQC = NCH * G
                OW = NQC * 128
                tiles = tile_list(g["order"])
                if DBG["tiles"] is not None:
                    tiles = tiles[:DBG["tiles"]]
                if not DBG["samp"]:
                    tiles = [t for t in tiles if t["kind"] != "samp"]
                zc0 = ZA0 if isA else ZB0
                need_z = isA or nm == "B0"

                S.dma(lambda e: e.dma_start(out=wq[:, :, 0:ncol],
                                            in_=w_in[:, col0:col0 + ncol].rearrange("(k p) n -> p k n", p=128)),
                      writes=[B_wq], eng="pool")
                S.dma(lambda e: e.dma_start(out=bias_bc[:, 0:ncol], in_=b_in[0, col0:col0 + ncol].partition_broadcast(128)),
                      writes=[B_bias])
                S.dma(lambda e: e.dma_start(out=cs_sb[:], in_=cs_d[pidx]), writes=[B_cs])
                if need_z:
                    for cq in range(NQC):
                        cp, gg = divmod(cq, G)
                        for e2 in range(2):
                            h = 2 * cp + e2
                            c_src = zc0 + h * G * 64 + gg * 64
                            S.dma(lambda e: e.dma_start(
                                out=wz[:, :, cq * 128 + e2 * 64: cq * 128 + e2 * 64 + 64],
                                in_=w_in[:, c_src:c_src + 64].rearrange("(k p) n -> p k n", p=128)),
                                writes=[B_wz], eng="pool")

                def stage0(ti, t):
                    issue_bg(1)
                    if phaseA:
                        issue_wconv(4)
                    yield from x_transpose(ti % 3, xT[ti % 2], B_xT[ti % 2], tr_rr)

                def stage1a(ti, t):
                    full = t["kind"] != "halo"
                    QKV, BQ = qkv[ti % 3], B_qkv[ti % 3]
                    XT, BXT = xT[ti % 2], B_xT[ti % 2]
                    c = 0 if full else nq
                    while c < ncol:
                        a0, a1 = c, min(c + 512, ncol)
                        c = a1
                        pacc, Bpa = acc_rr.next()
                        for k in range(16):
                            S.op("pe", lambda e: e.matmul(pacc[:, 0:a1 - a0], lhsT=XT[:, k, :], rhs=wq[:, k, a0:a1],
                                                          start=(k == 0), stop=(k == 15)), reads=[BXT, B_wq], writes=[Bpa])
                        S.op("dve", lambda e: e.tensor_tensor(out=QKV[:, a0:a1], in0=pacc[:, 0:a1 - a0], in1=bias_bc[:, a0:a1], op=ALU.add),
                             reads=[Bpa, B_bias], writes=[BQ])
                        yield

                def stage1z(ti, t):
                    full = t["kind"] != "halo"
                    XT, BXT = xT[ti % 2], B_xT[ti % 2]
                    if full and need_z:
                        ZT, BZT = zTs[ti % 3], B_zTs[ti % 3]
                        for c4 in range(0, NQC, 4):
                            pacc, Bpa = acc_rr.next()
                            for cq in range(c4, c4 + 4):
                                oc = (cq - c4) * 128
                                for k in range(16):
                                    S.op("pe", lambda e: e.matmul(pacc[:, oc:oc + 128], lhsT=wz[:, k, cq * 128:(cq + 1) * 128], rhs=XT[:, k, :],
                                                                  start=(k == 0), stop=False), reads=[B_wz, BXT], writes=[Bpa])
                                bo = (0 if isA else 1024) + cq * 128
                                S.op("pe", lambda e: e.matmul(pacc[:, oc:oc + 128], lhsT=bzr[:, bo:bo + 128], rhs=ones[:, 0:128],
                                                              start=False, stop=True), reads=[B_bzr, B_ones], writes=[Bpa])
                            S.op("act", lambda e: e.activation(out=ZT[:, c4:c4 + 4, :], in_=pacc[:].rearrange("p (a b) -> p a b", a=4),
                                                               func=AF.Silu), reads=[Bpa], writes=[BZT])
                            yield

                def stage1b_early(ti, t):
                    kind = t["kind"]
                    slot = t["slot"]
                    qb, B_qb, kb, B_kb = qbs[ti % 2], B_qbs[ti % 2], kbs[ti % 2], B_kbs[ti % 2]
                    full = kind != "halo"
                    QKV, BQ = qkv[ti % 3], B_qkv[ti % 3]
                    r0 = 0 if full else nq
                    nh = (nq + nk - r0) // 64
                    X = QKV[:, r0:nq + nk].rearrange("p (h d) -> p h d", d=64)
                    x1, x2 = X[:, :, 0:8], X[:, :, 8:16]
                    cosb = cs_sb[:, ti, 0:8].unsqueeze(1).to_broadcast([128, nh, 8])
                    sinb = cs_sb[:, ti, 8:16].unsqueeze(1).to_broadcast([128, nh, 8])
                    tt = [r[:, 0:nh, :] for r in rt]
                    for i_, (i0, i1) in enumerate(((x1, cosb), (x2, sinb), (x2, cosb), (x1, sinb))):
                        S.op("pool", lambda e: e.tensor_tensor(out=tt[i_], in0=i0, in1=i1, op=ALU.mult), reads=[BQ, B_cs], writes=[B_rts[i_]])
                    S.op("pool", lambda e: e.tensor_tensor(out=x1, in0=tt[0], in1=tt[1], op=ALU.subtract), reads=[B_rts[0], B_rts[1]], writes=[BQ])
                    S.op("pool", lambda e: e.tensor_tensor(out=x2, in0=tt[2], in1=tt[3], op=ALU.add), reads=[B_rts[2], B_rts[3]], writes=[BQ])
                    kv_src = QKV[:, nq:nq + 2 * nk]
                    if kind == "main":
                        pr = g["pst_rows"]
                        if t["col0"] >= CH - pr:
                            ro = t["col0"] - (CH - pr)
                            S.dma(lambda e: e.dma_start(out=pst[nm][sl(ro, 128, t["cstep"]), :], in_=kv_src), reads=[BQ])
                    elif kind == "samp":
                        for s in range(NSAMP):
                            S.dma(lambda e: e.dma_start(out=sst[nm][s, C - 8:C, :], in_=QKV[s * 8:(s + 1) * 8, nq:nq + 2 * nk]),
                                  reads=[BQ])
                    S.op("pool", lambda e: e.tensor_copy(out=kb[:, 0:nk], in_=QKV[:, nq:nq + nk]), reads=[BQ], writes=[B_kb])
                    S.op("pool", lambda e: e.tensor_copy(out=v_all[:, slot, 0:nk], in_=QKV[:, nq + nk:nq + 2 * nk]),
                         reads=[BQ], writes=[B_v[slot]])
                    if full:
                        for cp in range(NCH):
                            src = QKV[:, cp * 2 * G * 64:(cp + 1) * 2 * G * 64].rearrange("p (e g d) -> p g e d", e=2, g=G)
                            dst = qb[:, cp * G * 128:(cp + 1) * G * 128].rearrange("p (g e d) -> p g e d", g=G, e=2)
                            S.op("pool", lambda e: e.tensor_copy(out=dst, in_=src), reads=[BQ], writes=[B_qb])

                def stage1b_late(ti, t):
                    slot = t["slot"]
                    qb, B_qb, kb, B_kb = qbs[ti % 2], B_qbs[ti % 2], kbs[ti % 2], B_kbs[ti % 2]
                    full = t["kind"] != "halo"
                    for _ in pe_transposes([kb[:, cp * 128:(cp + 1) * 128] for cp in range(NCH)], [B_kb], tr_rr,
                                           lambda j0, n: kT_all[:, slot, j0:j0 + n, :], B_kT[slot]):
                        pass
                    if not full:
                        return
                    for _ in pe_transposes([qb[:, cq * 128:(cq + 1) * 128] for cq in range(NQC)], [B_qb], tr_rr,
                                           lambda j0, n: qT[ti % 2][:, j0:j0 + n, :], B_qT[ti % 2]):
                        pass

                def stage2(ti, t, par):
                    kind = t["kind"]
                    slot = t["slot"]
                    QT = qT[par]
                    BQT = B_qT[par]
                    started = set()

                    def pv(rhs_ap, v_ap, e2, out_fn, bank, reads, BPT):
                        for (ptile, Bp, lhs, rl) in ((p_ot, B_pot, v_ap, reads), (p_den, B_pden, ones[:, 0:64], [B_ones, BPT])):
                            key = (id(ptile), bank, e2)
                            first = key not in started
                            started.add(key)
                            S.op("pe", lambda e: e.matmul(out_fn(ptile), lhsT=lhs, rhs=rhs_ap, start=first, stop=True,
                                                          skip_group_check=True), reads=rl, writes=[Bp])

                    def qk_part(kslot, mask_off, hf):
                        kts = kT_all[:, kslot]
                        PT = PTs[pt_rr[0] % 2]
                        BPT = B_PTs[pt_rr[0] % 2]
                        pt_rr[0] += 1
                        if isA:
                            for e2 in range(2):
                                S.op("pe", lambda e: e.matmul(
                                    p_st[:, e2 * 512:(e2 + 1) * 512], lhsT=kts[64 * e2:64 * e2 + 64, 0, :],
                                    rhs=QT[64 * e2:64 * e2 + 64, hf * 4:(hf + 1) * 4, :], start=True, stop=True),
                                    reads=[B_kT[kslot], BQT], writes=[B_pst])
                        else:
                            for h in range(4):
                                cp, e2 = divmod(h, 2)
                                S.op("pe", lambda e: e.matmul(
                                    p_st[:, e2 * 512 + cp * 256:e2 * 512 + cp * 256 + 256], lhsT=kts[64 * e2:64 * e2 + 64, cp, :],
                                    rhs=QT[64 * e2:64 * e2 + 64, cp * 2:cp * 2 + 2, :], start=True, stop=True),
                                    reads=[B_kT[kslot], BQT], writes=[B_pst])
                        S.op("act", lambda e: e.activation(out=PT[:], in_=p_st[:], func=AF.Exp, scale=SCALE), reads=[B_pst], writes=[BPT])
                        PT3 = PT[:].rearrange("p (a b) -> p a b", a=8)
                        S.op("dve", lambda e: e.tensor_tensor(
                            out=PT3, in0=PT3, in1=maskt[:, mask_off:mask_off + 128].unsqueeze(1).to_broadcast([128, 8, 128]),
                            op=ALU.mult), reads=[BPT, B_mask], writes=[BPT])
                        return PT, BPT

                    def pv_part(kslot, hf, PT, BPT):
                        if isA:
                            for e2 in range(2):
                                pv(PT[:, e2 * 512:(e2 + 1) * 512], v_all[:, kslot, 64 * e2:64 * e2 + 64], e2,
                                   (lambda p, e2=e2, hf=hf: p[64 * e2:64 * e2 + 64, hf * 512:(hf + 1) * 512]), hf,
                                   [B_v[kslot], BPT], BPT)
                        else:
                            for h in range(4):
                                cp, e2 = divmod(h, 2)
                                pv(PT[:, e2 * 512 + cp * 256:e2 * 512 + cp * 256 + 256], v_all[:, kslot, 64 * h:64 * h + 64], e2,
                                   (lambda p, cp=cp, e2=e2: p[64 * e2:64 * e2 + 64, cp * 256:(cp + 1) * 256]), 0,
                                   [B_v[kslot], BPT], BPT)

                    def blocks_attn(blist):
                        subs = [(ks, mo, hf) for (ks, mo) in blist for hf in range(2 if isA else 1)]
                        prev_ = None
                        for sub in subs + [None]:
                            cur_ = None
                            if sub is not None:
                                PT, BPT = qk_part(*sub)
                                cur_ = (sub[0], sub[2], PT, BPT)
                                yield
                            if prev_ is not None:
                                pv_part(*prev_)
                                yield
                            prev_ = cur_

                    if kind == "main":
                        if isA:
                            moff = M_PA_H if t["prev_halo"] else M_PA
                        else:
                            moff = M_PB_H if t["prev_halo"] else M_PB
                        yield from blocks_attn([(slot, M_CUR), (t["prev"], moff)])
                    else:
                        yield from blocks_attn([(slot, {1: M_S1, 4: M_S4, 16: M_S16}[dil])])
                        nblk = {1: 1, 4: 4, 16: 8}[dil]
                        nt = {1: 8, 4: 2, 16: 1}[dil]
                        W = G * nt
                        half = nblk * NCH * W
                        def cache_load(s):
                            cbk, B_cbk = cbks[s % 2], B_cbks[s % 2]
                            if dil == 1:
                                S.dma(lambda e: e.dma_start(out=cbk[:, 0, 0:2 * nk], in_=cache[nm][s]), writes=[B_cbk], eng="pool")
                            elif dil == 4:
                                S.dma(lambda e: e.dma_start(out=cbk[:, 0:4, :], in_=cache[nm][s].rearrange("(m r) c -> m r c", r=4)),
                                      writes=[B_cbk], eng="pool")
                            else:
                                S.dma(lambda e: e.dma_start(out=cbk[:, 0:8, :],
                                                            in_=cache[nm][s].rearrange("(m r) c -> m r c", r=16)[:, 0:8, :]),
                                      writes=[B_cbk], eng="pool")
                        cache_load(0)
                        for s in range(NSAMP):
                            cbk = cbks[s % 2]
                            B_cbk = B_cbks[s % 2]
                            if s + 1 < NSAMP:
                                cache_load(s + 1)
                            items = [(b_, cp) for b_ in range(nblk) for cp in range(NCH)]
                            kTc_flat = kTc[:].rearrange("p a c b -> p (a c) b")
                            for _ in pe_transposes([cbk[:, b_, cp * 128:(cp + 1) * 128] for (b_, cp) in items], [B_cbk], tr_rr,
                                                   lambda j0, n: kTc_flat[:, j0:j0 + n, :], B_kTc):
                                pass

                            def tokc(b_):
                                if dil == 1:
                                    return slice(s * 8, s * 8 + 8)
                                if dil == 4:
                                    return slice(s * 8 + b_, s * 8 + b_ + 5, 4)
                                return slice(s * 8 + b_, s * 8 + b_ + 1)
                            PT = PTs[pt_rr[0] % 2]
                            BPT = B_PTs[pt_rr[0] % 2]
                            pt_rr[0] += 1
                            for b_ in range(nblk):
                                for h in range(Hkv):
                                    cp, e2 = divmod(h, 2)
                                    o0 = e2 * 512 + (b_ * NCH + cp) * W
                                    tk = tokc(b_)
                                    S.op("pe", lambda e: e.matmul(
                                        p_st[:, o0:o0 + W].rearrange("p (g t) -> p g t", g=G), lhsT=kTc_flat[64 * e2:64 * e2 + 64, b_ * NCH + cp, :],
                                        rhs=QT[64 * e2:64 * e2 + 64, cp * G:(cp + 1) * G, tk], start=True, stop=True),
                                        reads=[B_kTc, BQT], writes=[B_pst])
                            PTh = PT[:].rearrange("p (e c) -> p e c", e=2)[:, :, 0:half]
                            S.op("act", lambda e: e.activation(out=PTh, in_=p_st[:].rearrange("p (e c) -> p e c", e=2)[:, :, 0:half],
                                                               func=AF.Exp, scale=SCALE), reads=[B_pst], writes=[BPT])
                            if dil != 16:
                                moff = M_CA if isA else (M_CB0 if dil == 1 else M_CB1)
                                PTm = PTh.rearrange("p e (a t) -> p e a t", t=nt)
                                S.op("dve", lambda e: e.tensor_tensor(
                                    out=PTm, in0=PTm,
                                    in1=maskt[:, moff:moff + nt].unsqueeze(1).unsqueeze(1).to_broadcast([128, 2, half // nt, nt]),
                                    op=ALU.mult), reads=[BPT, B_mask], writes=[BPT])
                            yield
                            for b_ in range(nblk):
                                for h in range(Hkv):
                                    cp, e2 = divmod(h, 2)
                                    o0 = e2 * 512 + (b_ * NCH + cp) * W
                                    tk = tokc(b_)
                                    gs = 4 if isA else G
                                    for g0 in range(0, G, gs):
                                        c_lo_ = (cp * G + g0) * 128
                                        pv(PT[:, o0 + g0 * nt:o0 + (g0 + gs) * nt].rearrange("p (g t) -> p g t", g=gs),
                                           cbk[:, b_, nk + 64 * h:nk + 64 * h + 64], e2,
                                           (lambda p, e2=e2, tk=tk, c_lo_=c_lo_, gs=gs: p[64 * e2:64 * e2 + 64, c_lo_:c_lo_ + gs * 128]
                                            .rearrange("p (g q) -> p g q", g=gs)[:, :, tk]), c_lo_ // 512,
                                           [B_cbk, BPT], BPT)
                            yield

                    cols = sl(t["col0"], 128, t["cstep"])
                    if nm in ("B2", "B1"):
                        aN = accN[:, :, cols]
                        aD = accD[:, :, cols]
                        o3 = p_ot[:, 0:512].rearrange("p (a b) -> p a b", a=4)
                        d3 = p_den[:, 0:512].rearrange("p (a b) -> p a b", a=4)
                        if nm == "B2":
                            S.op("act", lambda e: e.activation(out=aN, in_=o3, func=AF.Copy), reads=[B_pot], writes=[B_accN])
                            S.op("dve", lambda e: e.tensor_copy(out=aD, in_=d3), reads=[B_pden], writes=[B_accD])
                        else:
                            S.op("dve", lambda e: e.tensor_tensor(out=aN, in0=o3, in1=aN, op=ALU.add), reads=[B_pot, B_accN], writes=[B_accN])
                            S.op("dve", lambda e: e.tensor_tensor(out=aD, in0=d3, in1=aD, op=ALU.add), reads=[B_pden, B_accD], writes=[B_accD])
                        return
                    dsum, B_dsum, nsum, B_nsum = dsums[ti % 2], B_dsums[ti % 2], nsums[ti % 2], B_nsums[ti % 2]
                    n3 = nsum[:, 0:OW].rearrange("p (a b) -> p a b", a=NQC)
                    d3s = dsum[:, 0:OW].rearrange("p (a b) -> p a b", a=NQC)
                    o3 = p_ot[:, 0:OW].rearrange("p (a b) -> p a b", a=NQC)
                    d3 = p_den[:, 0:OW].rearrange("p (a b) -> p a b", a=NQC)
                    if isA:
                        S.op("dve", lambda e: e.tensor_tensor(out=d3s, in0=d3, in1=sinkexp[:].unsqueeze(2).to_broadcast([128, 8, 128]),
                                                              op=ALU.add), reads=[B_pden, B_sink], writes=[B_dsum])
                        S.op("act", lambda e: e.activation(out=nsum[:, 0:OW], in_=p_ot[:, 0:OW], func=AF.Copy), reads=[B_pot], writes=[B_nsum])
                    else:
                        S.op("dve", lambda e: e.tensor_tensor(out=d3s, in0=d3, in1=accD[:, :, cols], op=ALU.add),
                             reads=[B_pden, B_accD], writes=[B_dsum])
                        S.op("dve", lambda e: e.tensor_tensor(out=n3, in0=o3, in1=accN[:, :, cols], op=ALU.add),
                             reads=[B_pot, B_accN], writes=[B_nsum])
                    S.op("act", lambda e: e.activation(out=dsum[:, 0:OW], in_=dsum[:, 0:OW], func=AF.Ln), reads=[B_dsum], writes=[B_dsum])
                    S.op("act", lambda e: e.activation(out=dsum[:, 0:OW], in_=dsum[:, 0:OW], func=AF.Exp, scale=-1.0),
                         reads=[B_dsum], writes=[B_dsum])
                    S.op("dve", lambda e: e.tensor_tensor(out=nsum[:, 0:OW], in0=nsum[:, 0:OW], in1=dsum[:, 0:OW], op=ALU.mult),
                         reads=[B_nsum, B_dsum], writes=[B_nsum])
                    ytile, B_yt = ytiles[ti % 2], B_yts[ti % 2]
                    S.op("dve", lambda e: e.tensor_tensor(out=ytile[:, 0:NQC, :], in0=n3, in1=zTs[ti % 3][:, 0:NQC, :], op=ALU.mult),
                         reads=[B_nsum, B_zTs[ti % 3]], writes=[B_yt])
                    ydst = ya_s if isA else yb_s
                    S.dma(lambda e: e.dma_start(out=ydst[:, :, cols], in_=ytile[:, 0:NQC, :]), reads=[B_yt])
                    yield

                nT = len(tiles)
                x_load(tiles[0]["row0"], tiles[0]["step"], 0)
                if nT > 1:
                    x_load(tiles[1]["row0"], tiles[1]["step"], 1)
                for r in range(nT + 3):
                    if r + 2 < nT:
                        x_load(tiles[r + 2]["row0"], tiles[r + 2]["step"], (r + 2) % 3)
                    t0_, t1a, t1b, t2 = r, r - 1, r - 2, r - 3
                    g2 = None
                    if 0 <= t2 < nT and tiles[t2]["kind"] != "halo" and DBG["attn"]:
                        g2 = stage2(t2, tiles[t2], t2 % 2)
                    g1a = stage1a(t1a, tiles[t1a]) if 0 <= t1a < nT else None
                    if 0 <= t1b < nT:
                        stage1b_early(t1b, tiles[t1b])
                    if g2 is not None:
                        try:
                            next(g2)
                        except StopIteration:
                            g2 = None
                    g0 = stage0(t0_, tiles[t0_]) if 0 <= t0_ < nT else None
                    run_interleaved([g1a, g2, g0])
                    if 0 <= t1a < nT:
                        for _ in stage1z(t1a, tiles[t1a]):
                            pass
                    if 0 <= t1b < nT:
                        stage1b_late(t1b, tiles[t1b])

        bgroups = [g for g in GROUPS if g["name"] != "A" and (DBG["groups"] is None or g["name"] in DBG["groups"])]
        agroups = [g for g in GROUPS if g["name"] == "A" and (DBG["groups"] is None or g["name"] in DBG["groups"])]
        with ExitStack() as ph:
            attention_phase(bgroups, ph, False)
        S.barrier()
        with ExitStack() as ph:
            attention_phase(agroups, ph, True)
        issue_bg(len(bg_pending))
        issue_wconv(len(wconv_pending))
        S.barrier()

        with ExitStack() as ph:
            def psum(name, shp, dt=F32):
                uniq[0] += 1
                return ph.enter_context(nc.psum_tensor(f"{name}_{uniq[0]}", shp, dt)), Buf()
            p_ga, B_pga = psum("p_ga", [128, 512])
            p_gb, B_pgb = psum("p_gb", [128, 512])
            p_ma, B_pma = psum("p_ma", [128, 512])
            p_mb, B_pmb = psum("p_mb", [128, 512])
            acc_rr = RR([psum("p_acc0", [128, 512]), psum("p_acc1", [128, 512])])
            tr_rr = RR([psum("p_tr0", [128, 512]), psum("p_tr1", [128, 512])])
            NH = 768
            mTh = sb(ph, "mTh", [128, 16, NH], BF16); B_mTh = Buf()
            lng = sb(ph, "lng", [128, DM], F32); B_lng = Buf()
            lnb = sb(ph, "lnb", [128, DM], F32); B_lnb = Buf()
            S.dma(lambda e: e.dma_start(out=lng[:], in_=ln_g[0, :].partition_broadcast(128)), writes=[B_lng])
            S.dma(lambda e: e.dma_start(out=lnb[:], in_=ln_b[0, :].partition_broadcast(128)), writes=[B_lnb])

            halves = [(0, 6), (6, 12), (12, 17)] if DBG["out"] else []
            for hi, (t0, t1) in enumerate(halves):
                nt_h = t1 - t0
                ntok = 128 * nt_h
                tok0 = 128 * t0
                with ExitStack() as s1:
                    xTh = sb(s1, "xTh", [128, 16, NH], BF16); B_xTh = Buf()
                    yaq = sb(s1, "yaq", [128, 8, NH], BF16); B_yaq = Buf()
                    ybq = sb(s1, "ybq", [128, 4, NH], BF16); B_ybq = Buf()
                    wch = [sb(s1, f"wch{i}", [128, WCH], BF16) for i in range(2)]; B_wch = [Buf(), Buf()]
                    sa = [sb(s1, f"sa{i}", [128, 512], F32) for i in range(2)]; B_sa = [Buf(), Buf()]
                    sbg = [sb(s1, f"sbg{i}", [128, 512], F32) for i in range(2)]; B_sbg = [Buf(), Buf()]
                    tA = [sb(s1, f"tA{i}", [128, 512], F32) for i in range(2)]; B_tA = [Buf(), Buf()]
                    tB = [sb(s1, f"tB{i}", [128, 512], F32) for i in range(2)]; B_tB = [Buf(), Buf()]
                    S.dma(lambda e: e.dma_start(out=yaq[:, :, 0:ntok], in_=ya_s[:, :, tok0:tok0 + ntok]), writes=[B_yaq])
                    S.dma(lambda e: e.dma_start(out=ybq[:, :, 0:ntok], in_=yb_s[:, :, tok0:tok0 + ntok]), writes=[B_ybq])
                    for tl in range(min(2, nt_h)):
                        x_load(2048 + 128 * (t0 + tl), 1, tl % 3)
                    for tl in range(nt_h):
                        if tl + 2 < nt_h:
                            x_load(2048 + 128 * (t0 + tl + 2), 1, (tl + 2) % 3)
                        for _ in x_transpose(tl % 3, xTh[:, :, tl * 128:(tl + 1) * 128], B_xTh, tr_rr):
                            pass
                    tgs = [(n0, min(n0 + 512, ntok)) for n0 in range(0, ntok, 512)]
                    it = 0
                    def wviews(cg2):
                        w_ = wch[cg2 % 2]
                        return (w_, B_wch[cg2 % 2],
                                w_[:, 0:4096].rearrange("p (k n) -> p k n", k=16),
                                w_[:, 4096:8192].rearrange("p (k n) -> p k n", k=16),
                                w_[:, 8192:10240].rearrange("p (k n) -> p k n", k=8),
                                w_[:, 10240:11264].rearrange("p (k n) -> p k n", k=4))

                    def wload(cg2):
                        w_, Bw, wga2, wgb2, wba2, wbb2 = wviews(cg2)
                        S.dma(lambda e: e.dma_start(out=w_[:], in_=wsc[cg2]), reads=[B_wsc[cg2]], writes=[Bw])

                    wload(0)
                    for c in range(16):
                        cg2, ci = divmod(c, 2)
                        w_, Bw, wga2, wgb2, wba2, wbb2 = wviews(cg2)
                        wga = wga2[:, :, ci * 128:(ci + 1) * 128]
                        wgb = wgb2[:, :, ci * 128:(ci + 1) * 128]
                        wba = wba2[:, :, ci * 128:(ci + 1) * 128]
                        wbb = wbb2[:, :, ci * 128:(ci + 1) * 128]
                        if ci == 0 and cg2 + 1 < 8:
                            wload(cg2 + 1)
                        for (n0, n1) in tgs:
                            n = n1 - n0
                            j = it % 2
                            it += 1
                            for k in range(16):
                                S.op("pe", lambda e: e.matmul(p_ga[:, 0:n], lhsT=wga[:, k, :], rhs=xTh[:, k, n0:n1],
                                                              start=(k == 0), stop=(k == 15)), reads=[Bw, B_xTh], writes=[B_pga])
                            S.op("act", lambda e: e.activation(out=sa[j][:, 0:n], in_=p_ga[:, 0:n], func=AF.Sigmoid,
                                                               bias=bfm[:, 12 + c:13 + c]), reads=[B_pga, B_bfm], writes=[B_sa[j]])
                            for k in range(16):
                                S.op("pe", lambda e: e.matmul(p_gb[:, 0:n], lhsT=wgb[:, k, :], rhs=xTh[:, k, n0:n1],
                                                              start=(k == 0), stop=(k == 15)), reads=[Bw, B_xTh], writes=[B_pgb])
                            S.op("act", lambda e: e.activation(out=sbg[j][:, 0:n], in_=p_gb[:, 0:n], func=AF.Sigmoid,
                                                               bias=bfm[:, 28 + c:29 + c]), reads=[B_pgb, B_bfm], writes=[B_sbg[j]])
                            for k in range(8):
                                S.op("pe", lambda e: e.matmul(p_ma[:, 0:n], lhsT=wba[:, k, :], rhs=yaq[:, k, n0:n1],
                                                              start=(k == 0), stop=(k == 7)), reads=[Bw, B_yaq], writes=[B_pma])
                            for k in range(4):
                                S.op("pe", lambda e: e.matmul(p_mb[:, 0:n], lhsT=wbb[:, k, :], rhs=ybq[:, k, n0:n1],
                                                              start=(k == 0), stop=(k == 3)), reads=[Bw, B_ybq], writes=[B_pmb])
                            S.op("dve", lambda e: e.tensor_tensor(out=tA[j][:, 0:n], in0=p_ma[:, 0:n], in1=sa[j][:, 0:n], op=ALU.mult),
                                 reads=[B_pma, B_sa[j]], writes=[B_tA[j]])
                            S.op("dve", lambda e: e.tensor_tensor(out=tB[j][:, 0:n], in0=p_mb[:, 0:n], in1=sbg[j][:, 0:n], op=ALU.mult),
                                 reads=[B_pmb, B_sbg[j]], writes=[B_tB[j]])
                            S.op("pool", lambda e: e.tensor_tensor(out=mTh[:, c, n0:n1], in0=tA[j][:, 0:n], in1=tB[j][:, 0:n], op=ALU.add),
                                 reads=[B_tA[j], B_tB[j]], writes=[B_mTh])
                S.barrier()
                with ExitStack() as s2:
                    wo = sb(s2, "wo", [128, 16, DM], BF16); B_wo = [Buf() for _ in range(4)]
                    hs = [sb(s2, f"hs{i}", [128, DM], F32) for i in range(2)]; B_hs = [Buf(), Buf()]
                    hn = [sb(s2, f"hn{i}", [128, DM], F32) for i in range(2)]; B_hn = [Buf(), Buf()]
                    stats = sb(s2, "stats", [128, 4, 6], F32); B_stats = Buf()
                    mv = sb(s2, "mv", [128, 4], F32); B_mv = Buf()
                    for cg in range(4):
                        S.dma(lambda e: e.dma_start(out=wo[:, :, cg * 512:(cg + 1) * 512], in_=wosc[:, :, cg * 512:(cg + 1) * 512]),
                              reads=[B_wosc[cg]], writes=[B_wo[cg]])
                    mo = [sb(s2, f"mo{i}", [128, DM], F32) for i in range(2)]; B_mo = [Buf(), Buf()]

                    def F1(tl):
                        j = tl % 2
                        row0 = 2048 + 128 * (t0 + tl)
                        S.dma(lambda e: e.dma_start(out=hs[j][:], in_=xe[row0:row0 + 128, :]), writes=[B_hs[j]])
                        for cg in range(4):
                            pacc, Bpa = acc_rr.next()
                            for k in range(16):
                                S.op("pe", lambda e: e.matmul(pacc[:], lhsT=mTh[:, k, tl * 128:(tl + 1) * 128], rhs=wo[:, k, cg * 512:(cg + 1) * 512],
                                                              start=(k == 0), stop=(k == 15)), reads=[B_mTh, B_wo[cg]], writes=[Bpa])
                            S.op("act", lambda e: e.activation(out=mo[j][:, cg * 512:(cg + 1) * 512], in_=pacc[:], func=AF.Copy),
                                 reads=[Bpa], writes=[B_mo[j]])

                    def F2(tl):
                        j = tl % 2
                        S.op("dve", lambda e: e.scalar_tensor_tensor(out=hs[j][:], in0=hs[j][:], scalar=ALPHA, in1=mo[j][:],
                                                                    op0=ALU.mult, op1=ALU.add), reads=[B_hs[j], B_mo[j]], writes=[B_hs[j]])
                        for i4 in range(4):
                            S.op("dve", lambda e: e.bn_stats(out=stats[:, i4, :], in_=hs[j][:, i4 * 512:(i4 + 1) * 512]),
                                 reads=[B_hs[j]], writes=[B_stats])
                        S.op("dve", lambda e: e.bn_aggr(out=mv[:, 0:2], in_=stats[:].rearrange("p a b -> p (a b)")), reads=[B_stats], writes=[B_mv])
                        S.op("act", lambda e: e.activation(out=mv[:, 2:3], in_=mv[:, 1:2], func=AF.Sqrt, bias=LN_EPS), reads=[B_mv], writes=[B_mv])
                        S.op("dve", lambda e: e.reciprocal(out=mv[:, 2:3], in_=mv[:, 2:3]), reads=[B_mv], writes=[B_mv])
                        S.op("dve", lambda e: e.tensor_tensor(out=mv[:, 3:4], in0=mv[:, 0:1], in1=mv[:, 2:3], op=ALU.mult), reads=[B_mv], writes=[B_mv])
                        S.op("dve", lambda e: e.tensor_scalar(out=mv[:, 3:4], in0=mv[:, 3:4], scalar1=-1.0, scalar2=None, op0=ALU.mult),
                             reads=[B_mv], writes=[B_mv])
                        S.op("pool", lambda e: e.tensor_scalar(out=hn[j][:], in0=hs[j][:], scalar1=mv[:, 2:3], scalar2=mv[:, 3:4],
                                                              op0=ALU.mult, op1=ALU.add), reads=[B_hs[j], B_mv], writes=[B_hn[j]])
                        S.op("dve", lambda e: e.tensor_tensor(out=hn[j][:], in0=hn[j][:], in1=lng[:], op=ALU.mult), reads=[B_hn[j], B_lng], writes=[B_hn[j]])
                        S.op("pool", lambda e: e.tensor_tensor(out=hn[j][:], in0=hn[j][:], in1=lnb[:], op=ALU.add), reads=[B_hn[j], B_lnb], writes=[B_hn[j]])
                        r0 = 128 * (t0 + tl)
                        S.dma(lambda e: e.dma_start(out=y_d[r0:r0 + 128, :], in_=hn[j][:]), reads=[B_hn[j]])

                    F1(0)
                    for tl in range(nt_h):
                        if tl + 1 < nt_h:
                            F1(tl + 1)
                        F2(tl)
                S.barrier()
        S.barrier()
        S.emit()
    return nc


def _masks(halo_valid):
    k = np.arange(128)[:, None]
    q = np.arange(128)[None, :]
    m = np.zeros((128, NMASK), np.float32)
    m[:, M_CUR:M_CUR + 128] = (k <= q)
    m[:, M_PA:M_PA + 128] = (k > q)
    m[:, M_PB:M_PB + 128] = (k >= q)
    m[:, M_PA_H:M_PA_H + 128] = (k > q) * halo_valid
    m[:, M_PB_H:M_PB_H + 128] = (k >= q) * halo_valid
    ss, ts = k // 8, k % 8
    sq, tq = q // 8, q % 8
    same = (ss == sq) & (ts <= tq)
    m[:, M_S1:M_S1 + 128] = same
    m[:, M_S4:M_S4 + 128] = same & ((tq - ts) % 4 == 0)
    m[:, M_S16:M_S16 + 128] = same & (tq == ts)
    t8 = np.arange(8)[None, :]
    m[:, M_CA:M_CA + 8] = (k >= t8 + 1)
    m[:, M_CB0:M_CB0 + 8] = (k >= t8)
    m[:, M_CB1] = 1.0
    m[:, M_CB1 + 1] = (k[:, 0] >= 1)
    m[:, M_ID:M_ID + 128] = np.eye(128)
    return m


def _rope_tables(start):
    inv = np.power(np.float32(THETA), -(np.arange(0, 16, 2, dtype=np.float32) / np.float32(16))).astype(np.float32)
    cs = np.zeros((4, 128, 33, 16), np.float32)
    p = np.arange(128)
    for gi, g in enumerate(GROUPS):
        for ti, t in enumerate(tile_list(g["order"])):
            if t["kind"] == "samp":
                pos = PAST + (p % 8)
            else:
                pos = (start - 2048) + t["row0"] + t["step"] * p
            ang = pos.astype(np.float32)[:, None] * inv[None, :]
            cs[gi, :, ti, 0:8] = np.cos(ang)
            cs[gi, :, ti, 8:16] = np.sin(ang)
    return cs


_NC_CACHE = {}


def kernel(x_prompt, x_sample, cache_a_kv, cache_b0_kv, cache_b1_kv, cache_b2_kv,
           w_in, b_in, sink_a, w_br_a, w_br_b, w_out, ln_g, ln_b):
    f32 = np.float32
    x_prompt = np.asarray(x_prompt, f32); x_sample = np.asarray(x_sample, f32)
    w_in2 = np.ascontiguousarray(np.asarray(w_in, f32)[0])
    b_in2 = np.ascontiguousarray(np.asarray(b_in, f32))
    bflat = b_in2[0]
    p = np.arange(128)
    e_, d_ = p // 64, p % 64
    bfm = np.zeros((128, 44), f32)
    for g in range(8):
        bfm[:, g] = bflat[ZA0 + e_ * 512 + g * 64 + d_]
    for cq in range(4):
        cp, gg = divmod(cq, 2)
        bfm[:, 8 + cq] = bflat[ZB0 + (2 * cp + e_) * 128 + gg * 64 + d_]
    for c in range(16):
        bfm[:, 12 + c] = bflat[GA0 + 128 * c + p]
        bfm[:, 28 + c] = bflat[GB0 + 128 * c + p]
    sk = np.asarray(sink_a, f32)[0]
    sinkl = np.zeros((128, 8), f32)
    for g in range(8):
        sinkl[:, g] = sk[e_ * 8 + g]
    caches = {"cache_a": np.asarray(cache_a_kv, f32)[0].reshape(128, 128, 256),
              "cache_b0": np.asarray(cache_b0_kv, f32)[0].reshape(128, 128, 512),
              "cache_b1": np.asarray(cache_b1_kv, f32)[0].reshape(128, 512, 512),
              "cache_b2": np.asarray(cache_b2_kv, f32)[0].reshape(128, 2048, 512)}
    bzrow = np.ascontiguousarray(bfm[:, 0:12].T.reshape(1, 1536))
    common = {"w_in": w_in2, "b_in": b_in2, "bfm": bfm, "bzrow": bzrow, "sinkl": sinkl,
              "w_br_a": np.ascontiguousarray(np.asarray(w_br_a, f32)[0]),
              "w_br_b": np.ascontiguousarray(np.asarray(w_br_b, f32)[0]),
              "w_out": np.ascontiguousarray(np.asarray(w_out, f32)[0]),
              "ln_g": np.ascontiguousarray(np.asarray(ln_g, f32)), "ln_b": np.ascontiguousarray(np.asarray(ln_b, f32))}
    in_maps = []
    for c in range(NCORE):
        n, j = divmod(c, 4)
        start = CH * j
        xe = np.zeros((4224, DM), f32)
        if j > 0:
            xe[0:2048] = x_prompt[n, start - 2048:start]
        xe[2048:4096] = x_prompt[n, start:start + CH]
        xe[4096:] = x_sample[NSAMP * c:NSAMP * (c + 1)].reshape(128, DM)
        m = dict(common)
        m["xe"] = xe
        m["cs"] = _rope_tables(start)
        m["masks"] = _masks(1.0 if j > 0 else 0.0)
        for k, v in caches.items():
            m[k] = np.ascontiguousarray(v[NSAMP * c:NSAMP * (c + 1)])
        in_maps.append(m)

    if "nc" not in _NC_CACHE:
        _NC_CACHE["nc"] = build_program()
    res = run_bass_kernel_spmd(_NC_CACHE["nc"], in_maps, core_ids=list(range(NCORE)))
    R = res.results

    y_prompt = np.zeros((2, 8192, DM), f32)
    y_sample = np.zeros((128, 8, DM), f32)
    for c in range(NCORE):
        n, j = divmod(c, 4)
        y_prompt[n, CH * j:CH * (j + 1)] = R[c]["y"][0:CH]
        y_sample[NSAMP * c:NSAMP * (c + 1)] = R[c]["y"][CH:].reshape(NSAMP, 8, DM)
    shp = {"a": (2, 2, 64), "b0": (2, 4, 64), "b1": (2, 4, 64), "b2": (2, 4, 64)}
    rows = {"a": 128, "b0": 128, "b1": 512, "b2": 2048}
    pouts, souts = [], []
    for key in ("a", "b0", "b1", "b2"):
        po = np.stack([R[4 * n + 3]["pst_" + key].reshape((rows[key],) + shp[key]) for n in range(2)], 0)[None]
        pouts.append(np.ascontiguousarray(po.astype(f32)))
        so = np.concatenate([R[c]["sst_" + key].reshape((NSAMP, rows[key]) + shp[key]) for c in range(NCORE)], 0)[None]
        souts.append(np.ascontiguousarray(so.astype(f32)))
    return (y_prompt, y_sample, pouts[0], pouts[1], pouts[2], pouts[3], souts[0], souts[1], souts[2], souts[3])
```

```python
import numpy as np
from contextlib import ExitStack
import concourse.bass as bass
import concourse.mybir as mybir
from concourse.bass_utils import run_bass_kernel_spmd

F32 = mybir.dt.float32
BF16 = mybir.dt.bfloat16
AF = mybir.ActivationFunctionType
ALU = mybir.AluOpType

NCORE = 8
DM = 2048
CH = 2048
NSAMP = 16
PAST = 16384
NTOK = CH + 128
ALPHA = 2.0 ** 0.25
LN_EPS = 1e-5
SCALE = 0.125
THETA = 500000.0

GROUPS = [
    dict(name="B2", col0=4352, nq=512, nk=256, Hkv=4, G=2, dil=16, C=2048, order="f16", pst_rows=2048),
    dict(name="B1", col0=3328, nq=512, nk=256, Hkv=4, G=2, dil=4, C=512, order="f4", pst_rows=512),
    dict(name="B0", col0=2304, nq=512, nk=256, Hkv=4, G=2, dil=1, C=128, order="nat", pst_rows=128),
    dict(name="A", col0=0, nq=1024, nk=128, Hkv=2, G=8, dil=1, C=128, order="nat", pst_rows=128),
]
ZA0, ZB0, GA0, GB0 = 1280, 5376, 5888, 7936

DBG = {"groups": None, "tiles": None, "out": True, "bgcopy": True, "samp": True, "attn": True}

M_CUR, M_PA_H, M_PB_H, M_PA, M_PB, M_S1, M_S4, M_S16 = 0, 128, 256, 384, 512, 640, 768, 896
M_CA, M_CB0, M_CB1, M_ID = 1024, 1032, 1040, 1056
NMASK = 1184


def sl(start, n, step):
    return slice(start, start + (n - 1) * step + 1, step)


def tile_list(order):
    tl = []
    if order == "nat":
        tl.append(dict(kind="halo", row0=2048 - 128, step=1, slot=0))
        for b in range(16):
            tl.append(dict(kind="main", row0=2048 + 128 * b, step=1, slot=1 + b % 3,
                           prev=(0 if b == 0 else 1 + (b - 1) % 3), prev_halo=(b == 0), col0=128 * b, cstep=1))
    elif order == "f4":
        for r in range(4):
            tl.append(dict(kind="halo", row0=2048 - 512 + r, step=4, slot=r))
        for t in range(16):
            M, r = divmod(t, 4)
            tl.append(dict(kind="main", row0=2048 + 512 * M + r, step=4, slot=4 + t % 8,
                           prev=(r if M == 0 else 4 + (t - 4) % 8), prev_halo=(M == 0), col0=512 * M + r, cstep=4))
    else:
        for r in range(16):
            tl.append(dict(kind="halo", row0=r, step=16, slot=r))
        for r in range(16):
            tl.append(dict(kind="main", row0=2048 + r, step=16, slot=16 + r % 2, prev=r, prev_halo=True, col0=r, cstep=16))
    tl.append(dict(kind="samp", row0=4096, step=1, slot={"nat": 2, "f4": 4, "f16": 16}[order], col0=2048, cstep=1))
    return tl


class Buf:
    __slots__ = ("last_w", "readers")

    def __init__(self):
        self.last_w = None
        self.readers = {}


class _Rec:
    def __getattr__(self, name):
        def f(*a, **kw):
            return (name, a, kw)
        return f


_REC = _Rec()


class Sched:
    ENGS = ("pe", "act", "dve", "pool", "sp")

    def __init__(self, nc, stack, n_dma_sems=90):
        self.nc = nc
        self.sems = {}
        for k in self.ENGS:
            self.sems[k] = stack.enter_context(nc.semaphore(f"sem_{k}"))
        self.n_dma = n_dma_sems
        for j in range(n_dma_sems):
            self.sems[f"d{j}"] = stack.enter_context(nc.semaphore(f"sem_d{j}"))
        self.sems["bg"] = stack.enter_context(nc.semaphore("sem_bg"))
        self.bg_val = 0
        self.dma_val = [0] * n_dma_sems
        self.dma_pool = {"sp": list(range(0, 44)), "pool": list(range(44, 86)), "act": list(range(86, n_dma_sems))}
        self.dma_rr = {"sp": 0, "pool": 0, "act": 0}
        self.cnt = {k: 0 for k in self.ENGS}
        self.prog = {k: [] for k in self.ENGS}
        self.waited = {k: {} for k in self.ENGS}

    def _deps(self, reads, writes):
        deps = {}

        def add(k, v):
            if deps.get(k, 0) < v:
                deps[k] = v
        for b in reads:
            if b.last_w is not None:
                add(*b.last_w)
        for b in writes:
            if b.last_w is not None:
                add(*b.last_w)
            for k, v in b.readers.items():
                add(k, v)
        return deps

    def _commit(self, tok, reads, writes):
        k, v = tok
        for b in reads:
            if b.readers.get(k, 0) < v:
                b.readers[k] = v
        for b in writes:
            b.last_w = tok
            b.readers = {}

    def _waits(self, eng, deps, skip_self_pe=True):
        waits = []
        w = self.waited[eng]
        for k, v in deps.items():
            if k == eng and eng == "pe" and skip_self_pe:
                continue
            if w.get(k, 0) >= v:
                continue
            w[k] = v
            waits.append((k, v))
        return waits

    def op(self, eng, fn, reads=(), writes=()):
        deps = self._deps(reads, writes)
        waits = self._waits(eng, deps)
        self.cnt[eng] += 1
        tok = (eng, self.cnt[eng])
        self.prog[eng].append((waits, fn(_REC), tok, 1))
        self._commit(tok, reads, writes)
        return tok

    def dma_bg(self, fn, eng="act"):
        self.bg_val += 16
        self.prog[eng].append(([], fn(_REC), ("bg", self.bg_val), 16))

    def dma(self, fn, reads=(), writes=(), eng="sp"):
        deps = self._deps(reads, writes)
        pl = self.dma_pool[eng]
        j = pl[self.dma_rr[eng] % len(pl)]
        self.dma_rr[eng] += 1
        key = f"d{j}"
        if self.dma_val[j] > 0 and deps.get(key, 0) < self.dma_val[j]:
            deps[key] = self.dma_val[j]
        waits = self._waits(eng, deps, skip_self_pe=False)
        self.dma_val[j] += 16
        tok = (key, self.dma_val[j])
        self.prog[eng].append((waits, fn(_REC), tok, 16))
        self._commit(tok, reads, writes)
        return tok

    def barrier(self):
        allw = [(k, self.cnt[k]) for k in self.ENGS if self.cnt[k] > 0]
        allw += [(f"d{j}", self.dma_val[j]) for j in range(self.n_dma) if self.dma_val[j] > 0]
        if self.bg_val > 0:
            allw.append(("bg", self.bg_val))
        for eng in self.ENGS:
            waits = []
            w = self.waited[eng]
            for k, v in allw:
                if k == eng:
                    continue
                if w.get(k, 0) >= v:
                    continue
                w[k] = v
                waits.append((k, v))
            self.prog[eng].append((waits, None, None, 0))

    def emit(self):
        nc, sems, prog = self.nc, self.sems, self.prog

        def run(e, items):
            for waits, fn, tok, inc in items:
                for k, v in waits:
                    e.wait_ge(sems[k], v)
                if fn is None:
                    continue
                ins = getattr(e, fn[0])(*fn[1], **fn[2])
                ins.then_inc(sems[tok[0]], inc)

        with nc.Block() as block:
            @block.sync
            def _(e):
                run(e, prog["sp"])

            @block.tensor
            def _(e):
                run(e, prog["pe"])

            @block.scalar
            def _(e):
                run(e, prog["act"])

            @block.vector
            def _(e):
                run(e, prog["dve"])

            @block.gpsimd
            def _(e):
                run(e, prog["pool"])


def build_program():
    nc = bass.Bass("TRN2", target_bir_lowering=False)

    def din(name, shape, dt=F32):
        return nc.dram_tensor(name, shape, dt, kind="ExternalInput").ap()

    def dout(name, shape, dt=F32):
        return nc.dram_tensor(name, shape, dt, kind="ExternalOutput").ap()

    xe = din("xe", [4224, DM])
    w_in = din("w_in", [DM, 9984])
    b_in = din("b_in", [1, 9984])
    bfm_d = din("bfm", [128, 44])
    bzrow_d = din("bzrow", [1, 1536])
    sink_d = din("sinkl", [128, 8])
    w_br_a = din("w_br_a", [1024, DM])
    w_br_b = din("w_br_b", [512, DM])
    w_out = din("w_out", [DM, DM])
    ln_g = din("ln_g", [1, DM])
    ln_b = din("ln_b", [1, DM])
    cs_d = din("cs", [4, 128, 33, 16])
    masks_d = din("masks", [128, NMASK])
    cache = {"A": din("cache_a", [NSAMP, 128, 256]), "B0": din("cache_b0", [NSAMP, 128, 512]),
             "B1": din("cache_b1", [NSAMP, 512, 512]), "B2": din("cache_b2", [NSAMP, 2048, 512])}
    y_d = dout("y", [NTOK, DM])
    pst = {"A": dout("pst_a", [128, 256]), "B0": dout("pst_b0", [128, 512]),
           "B1": dout("pst_b1", [512, 512]), "B2": dout("pst_b2", [2048, 512])}
    sst = {"A": dout("sst_a", [NSAMP, 128, 256]), "B0": dout("sst_b0", [NSAMP, 128, 512]),
           "B1": dout("sst_b1", [NSAMP, 512, 512]), "B2": dout("sst_b2", [NSAMP, 2048, 512])}
    ya_s = nc.dram_tensor("ya_s", [128, 8, NTOK], BF16, kind="Internal").ap()
    yb_s = nc.dram_tensor("yb_s", [128, 4, NTOK], BF16, kind="Internal").ap()

    with ExitStack() as st:
        S = Sched(nc, st)

        uniq = [0]

        def sb(stack, name, shp, dt):
            uniq[0] += 1
            return stack.enter_context(nc.sbuf_tensor(f"{name}_{uniq[0]}", shp, dt))

        maskt = sb(st, "maskt", [128, NMASK], BF16); B_mask = Buf()
        ones = sb(st, "ones", [128, 128], BF16); B_ones = Buf()
        bzr = sb(st, "bzr", [128, 1536], BF16); B_bzr = Buf()
        bfm = sb(st, "bfm_sb", [128, 44], F32); B_bfm = Buf()
        xb = [sb(st, f"xb{i}", [128, DM], BF16) for i in range(3)]; B_xb = [Buf(), Buf(), Buf()]
        ident = maskt[:, M_ID:M_ID + 128]

        S.dma(lambda e: e.dma_start(out=maskt[:], in_=masks_d), writes=[B_mask], eng="pool")
        S.dma(lambda e: e.dma_start(out=bfm[:], in_=bfm_d), writes=[B_bfm])
        S.op("pool", lambda e: e.memset(bzr[:], 0.0), writes=[B_bzr])
        S.dma(lambda e: e.dma_start(out=bzr[0:1, :], in_=bzrow_d), writes=[B_bzr], eng="pool")
        S.op("dve", lambda e: e.memset(ones[:], 1.0), writes=[B_ones])

        bg_pending = []
        for g in (GROUPS if DBG["bgcopy"] else []):
            nm, C = g["name"], g["C"]
            for s in range(NSAMP):
                bg_pending.append((nm, C, s))

        def issue_bg(n=1):
            for _ in range(n):
                if bg_pending:
                    nm, C, s = bg_pending.pop(0)
                    S.dma_bg(lambda e: e.dma_start(out=sst[nm][s, 0:C - 8, :], in_=cache[nm][s, 8:C, :]))

        WCH = 2 * (2048 + 2048 + 1024 + 512)
        wsc = nc.dram_tensor("wsc", [8, 128, WCH], BF16, kind="Internal").ap()
        B_wsc = [Buf() for _ in range(8)]
        wosc = nc.dram_tensor("wosc", [128, 16, DM], BF16, kind="Internal").ap()
        B_wosc = [Buf() for _ in range(4)]
        wconv_pending = []
        for cg2 in range(8):
            c0_ = 256 * cg2
            dst = wsc[cg2]
            wconv_pending.append((cg2, dst[:, 0:4096].rearrange("p (k n) -> p k n", k=16),
                                  w_in[:, GA0 + c0_:GA0 + c0_ + 256].rearrange("(k p) n -> p k n", p=128)))
            wconv_pending.append((cg2, dst[:, 4096:8192].rearrange("p (k n) -> p k n", k=16),
                                  w_in[:, GB0 + c0_:GB0 + c0_ + 256].rearrange("(k p) n -> p k n", p=128)))
            for e2 in range(2):
                wconv_pending.append((cg2, dst[64 * e2:64 * e2 + 64, 8192:10240].rearrange("p (k n) -> p k n", k=8),
                                      w_br_a[e2 * 512:(e2 + 1) * 512, c0_:c0_ + 256].rearrange("(g d) n -> d g n", d=64)))
                for cp in range(2):
                    wconv_pending.append((cg2, dst[64 * e2:64 * e2 + 64, 10240:11264].rearrange("p (k n) -> p k n", k=4)[:, cp * 2:cp * 2 + 2, :],
                                          w_br_b[(2 * cp + e2) * 128:(2 * cp + e2 + 1) * 128, c0_:c0_ + 256]
                                          .rearrange("(g d) n -> d g n", d=64)))

        for cg in range(4):
            wconv_pending.append((("o", cg), wosc[:, :, cg * 512:(cg + 1) * 512],
                                  w_out[:, cg * 512:(cg + 1) * 512].rearrange("(k p) n -> p k n", p=128)))

        def issue_wconv(n):
            for _ in range(n):
                if wconv_pending:
                    cg2, o_, i_ = wconv_pending.pop(0)
                    Bw_ = B_wosc[cg2[1]] if isinstance(cg2, tuple) else B_wsc[cg2]
                    S.dma(lambda e: e.dma_start(out=o_, in_=i_), writes=[Bw_], eng="pool")

        evac_flip = [0]

        def evac_copy(out_ap, in_ap, reads, writes):
            evac_flip[0] ^= 1
            if evac_flip[0]:
                S.op("act", lambda e: e.activation(out=out_ap, in_=in_ap, func=AF.Copy), reads=reads, writes=writes)
            else:
                S.op("dve", lambda e: e.tensor_copy(out=out_ap, in_=in_ap), reads=reads, writes=writes)

        class RR:
            def __init__(self, items):
                self.items = items
                self.i = 0

            def next(self):
                it = self.items[self.i % len(self.items)]
                self.i += 1
                return it

        def x_load(row0, step, i):
            S.dma(lambda e: e.dma_start(out=xb[i][:], in_=xe[sl(row0, 128, step), :]), writes=[B_xb[i]], eng="pool")

        def pe_transposes(srcs, reads, trr, dst_fn, dst_buf):
            for j0 in range(0, len(srcs), 4):
                grp = srcs[j0:j0 + 4]
                ptr, Bp = trr.next()
                for j, src in enumerate(grp):
                    S.op("pe", lambda e: e.matmul(ptr[:, j * 128:(j + 1) * 128], lhsT=src, rhs=ident, start=True, stop=True),
                         reads=reads + [B_mask], writes=[Bp])
                evac_copy(dst_fn(j0, len(grp)), ptr[:, 0:len(grp) * 128].rearrange("p (a b) -> p a b", a=len(grp)), [Bp], [dst_buf])
                yield

        def x_transpose(i, dst, dst_buf, trr):
            yield from pe_transposes([xb[i][:, k * 128:(k + 1) * 128] for k in range(16)], [B_xb[i]], trr,
                                     lambda j0, n: dst[:, j0:j0 + n, :], dst_buf)

        def run_interleaved(gens):
            alive = [g for g in gens if g is not None]
            while alive:
                for g in list(alive):
                    try:
                        next(g)
                    except StopIteration:
                        alive.remove(g)

        def attention_phase(groups, ph, phaseA):
            def psum(name, shp, dt=F32):
                uniq[0] += 1
                return ph.enter_context(nc.psum_tensor(f"{name}_{uniq[0]}", shp, dt)), Buf()
            p_st, B_pst = psum("p_st", [128, 1024])
            if phaseA:
                p_ot, B_pot = psum("p_ot", [128, 1024])
                p_den, B_pden = psum("p_den", [128, 1024])
                acc_rr = RR([psum("p_acc0", [128, 512])])
                tr_rr = RR([psum("p_tr0", [128, 512])])
            else:
                p_ot, B_pot = psum("p_ot", [128, 512])
                p_den, B_pden = psum("p_den", [128, 512])
                acc_rr = RR([psum("p_acc0", [128, 512]), psum("p_acc1", [128, 512])])
                tr_rr = RR([psum("p_tr0", [128, 512]), psum("p_tr1", [128, 512])])
            WQC = 1280 if phaseA else 1024
            WZC = 1024 if phaseA else 512
            wq = sb(ph, "wq", [128, 16, WQC], BF16); B_wq = Buf()
            wz = sb(ph, "wz", [128, 16, WZC], BF16); B_wz = Buf()
            bias_bc = sb(ph, "bias_bc", [128, WQC], F32); B_bias = Buf()
            cs_sbs = [sb(ph, f"cs_sb{i}", [128, 33, 16], F32) for i in range(2)]; B_css = [Buf(), Buf()]
            loaded_passes = set()

            def load_pass_weights(gi_):
                if gi_ in loaded_passes or gi_ >= len(groups):
                    return
                loaded_passes.add(gi_)
                g_ = groups[gi_]
                nm_, col0_, nq_, nk_, Hkv_, G_ = (g_[k] for k in ("name", "col0", "nq", "nk", "Hkv", "G"))
                ncol_ = nq_ + 2 * nk_
                pidx_ = [x["name"] for x in GROUPS].index(nm_)
                S.dma(lambda e: e.dma_start(out=wq[:, :, 0:ncol_],
                                            in_=w_in[:, col0_:col0_ + ncol_].rearrange("(k p) n -> p k n", p=128)),
                      writes=[B_wq], eng="pool")
                S.dma(lambda e: e.dma_start(out=bias_bc[:, 0:ncol_], in_=b_in[0, col0_:col0_ + ncol_].partition_broadcast(128)),
                      writes=[B_bias])
                S.dma(lambda e: e.dma_start(out=cs_sbs[gi_ % 2][:], in_=cs_d[pidx_]), writes=[B_css[gi_ % 2]])
                if nm_ in ("A", "B0"):
                    zc0_ = ZA0 if nm_ == "A" else ZB0
                    for cq in range((Hkv_ // 2) * G_):
                        cp, gg = divmod(cq, G_)
                        for e2 in range(2):
                            h = 2 * cp + e2
                            c_src = zc0_ + h * G_ * 64 + gg * 64
                            S.dma(lambda e: e.dma_start(
                                out=wz[:, :, cq * 128 + e2 * 64: cq * 128 + e2 * 64 + 64],
                                in_=w_in[:, c_src:c_src + 64].rearrange("(k p) n -> p k n", p=128)),
                                writes=[B_wz], eng="pool")
            kT_all = sb(ph, "kT_all", [128, 18, 2, 128], BF16); B_kT = [Buf() for _ in range(18)]
            v_all = sb(ph, "v_all", [128, 18, 256], BF16); B_v = [Buf() for _ in range(18)]
            xT = [sb(ph, f"xT{i}", [128, 16, 128], BF16) for i in range(2)]; B_xT = [Buf(), Buf()]
            qkv = [sb(ph, f"qkv{i}", [128, WQC], F32) for i in range(3)]; B_qkv = [Buf(), Buf(), Buf()]
            rt = [sb(ph, f"rt{i}", [128, 18, 8], F32) for i in range(4)]; B_rts = [Buf() for _ in range(4)]
            qb = sb(ph, "qb", [128, 1024], BF16); B_qb = Buf()
            kb = sb(ph, "kb", [128, 256], BF16); B_kb = Buf()
            qT = [sb(ph, f"qT{i}", [128, 8, 128], BF16) for i in range(2)]; B_qT = [Buf(), Buf()]
            PTs = [sb(ph, f"PT{i}", [128, 1024], BF16) for i in range(2)]; B_PTs = [Buf(), Buf()]
            pt_rr = [0]
            dsum = sb(ph, "dsum", [128, 1024 if phaseA else 512], F32); B_dsum = Buf()
            nsum = sb(ph, "nsum", [128, 1024 if phaseA else 512], F32); B_nsum = Buf()
            zTs = [sb(ph, f"zT{i}", [128, 8 if phaseA else 4, 128], BF16) for i in range(3)]; B_zTs = [Buf(), Buf(), Buf()]
            ytiles = [sb(ph, f"ytile{i}", [128, 8, 128], BF16) for i in range(2)]; B_yts = [Buf(), Buf()]
            cbks = [sb(ph, f"cbk{i}", [128, 8, 512], BF16) for i in range(2)]; B_cbks = [Buf(), Buf()]
            kTc = sb(ph, "kTc", [128, 8, 2, 128], BF16); B_kTc = Buf()
            if phaseA:
                sinkexp = sb(ph, "sinkexp", [128, 8], F32); B_sink = Buf()
                S.dma(lambda e: e.dma_start(out=sinkexp[:], in_=sink_d), writes=[B_sink])
                S.op("act", lambda e: e.activation(out=sinkexp[:], in_=sinkexp[:], func=AF.Exp), reads=[B_sink], writes=[B_sink])
            else:
                accN = sb(ph, "accN", [128, 4, NTOK], BF16); B_accN = Buf()
                accD = sb(ph, "accD", [128, 4, NTOK], BF16); B_accD = Buf()

            for gi, g in enumerate(groups):
                nm, col0, nq, nk, Hkv, G, dil, C = (g[k] for k in ("name", "col0", "nq", "nk", "Hkv", "G", "dil", "C"))
                isA = nm == "A"
                pidx = [x["name"] for x in GROUPS].index(nm)
                ncol = nq + 2 * nk
                NCH = Hkv // 2
                NQC = NCH * G
                OW = NQC * 128
                tiles = tile_list(g["order"])
                if DBG["tiles"] is not None:
                    tiles = tiles[:DBG["tiles"]]
                if not DBG["samp"]:
                    tiles = [t for t in tiles if t["kind"] != "samp"]
                zc0 = ZA0 if isA else ZB0
                need_z = isA or nm == "B0"

                load_pass_weights(gi)
                cs_sb = cs_sbs[gi % 2]
                B_cs = B_css[gi % 2]

                def stage0(ti, t):
                    issue_bg(1)
                    if phaseA:
                        issue_wconv(4)
                    yield from x_transpose(ti % 3, xT[ti % 2], B_xT[ti % 2], tr_rr)

                def stage1a(ti, t):
                    full = t["kind"] != "halo"
                    QKV, BQ = qkv[ti % 3], B_qkv[ti % 3]
                    XT, BXT = xT[ti % 2], B_xT[ti % 2]
                    c = 0 if full else nq
                    while c < ncol:
                        a0, a1 = c, min(c + 512, ncol)
                        c = a1
                        pacc, Bpa = acc_rr.next()
                        for k in range(16):
                            S.op("pe", lambda e: e.matmul(pacc[:, 0:a1 - a0], lhsT=XT[:, k, :], rhs=wq[:, k, a0:a1],
                                                          start=(k == 0), stop=(k == 15)), reads=[BXT, B_wq], writes=[Bpa])
                        S.op("dve", lambda e: e.tensor_tensor(out=QKV[:, a0:a1], in0=pacc[:, 0:a1 - a0], in1=bias_bc[:, a0:a1], op=ALU.add),
                             reads=[Bpa, B_bias], writes=[BQ])
                        yield

                def stage1z(ti, t):
                    full = t["kind"] != "halo"
                    XT, BXT = xT[ti % 2], B_xT[ti % 2]
                    if full and need_z:
                        ZT, BZT = zTs[ti % 3], B_zTs[ti % 3]
                        for c4 in range(0, NQC, 4):
                            pacc, Bpa = acc_rr.next()
                            for cq in range(c4, c4 + 4):
                                oc = (cq - c4) * 128
                                for k in range(16):
                                    S.op("pe", lambda e: e.matmul(pacc[:, oc:oc + 128], lhsT=wz[:, k, cq * 128:(cq + 1) * 128], rhs=XT[:, k, :],
                                                                  start=(k == 0), stop=False), reads=[B_wz, BXT], writes=[Bpa])
                                bo = (0 if isA else 1024) + cq * 128
                                S.op("pe", lambda e: e.matmul(pacc[:, oc:oc + 128], lhsT=bzr[:, bo:bo + 128], rhs=ones[:, 0:128],
                                                              start=False, stop=True), reads=[B_bzr, B_ones], writes=[Bpa])
                            S.op("act", lambda e: e.activation(out=ZT[:, c4:c4 + 4, :], in_=pacc[:].rearrange("p (a b) -> p a b", a=4),
                                                               func=AF.Silu), reads=[Bpa], writes=[BZT])
                            yield

                def stage1b_early(ti, t):
                    kind = t["kind"]
                    slot = t["slot"]
                    full = kind != "halo"
                    QKV, BQ = qkv[ti % 3], B_qkv[ti % 3]
                    r0 = 0 if full else nq
                    nh = (nq + nk - r0) // 64
                    X = QKV[:, r0:nq + nk].rearrange("p (h d) -> p h d", d=64)
                    x1, x2 = X[:, :, 0:8], X[:, :, 8:16]
                    cosb = cs_sb[:, ti, 0:8].unsqueeze(1).to_broadcast([128, nh, 8])
                    sinb = cs_sb[:, ti, 8:16].unsqueeze(1).to_broadcast([128, nh, 8])
                    tt = [r[:, 0:nh, :] for r in rt]
                    for i_, (i0, i1) in enumerate(((x1, cosb), (x2, sinb), (x2, cosb), (x1, sinb))):
                        S.op("pool", lambda e: e.tensor_tensor(out=tt[i_], in0=i0, in1=i1, op=ALU.mult), reads=[BQ, B_cs], writes=[B_rts[i_]])
                    S.op("pool", lambda e: e.tensor_tensor(out=x1, in0=tt[0], in1=tt[1], op=ALU.subtract), reads=[B_rts[0], B_rts[1]], writes=[BQ])
                    S.op("pool", lambda e: e.tensor_tensor(out=x2, in0=tt[2], in1=tt[3], op=ALU.add), reads=[B_rts[2], B_rts[3]], writes=[BQ])
                    kv_src = QKV[:, nq:nq + 2 * nk]
                    if kind == "main":
                        pr = g["pst_rows"]
                        if t["col0"] >= CH - pr:
                            ro = t["col0"] - (CH - pr)
                            S.dma(lambda e: e.dma_start(out=pst[nm][sl(ro, 128, t["cstep"]), :], in_=kv_src), reads=[BQ])
                    elif kind == "samp":
                        for s in range(NSAMP):
                            S.dma(lambda e: e.dma_start(out=sst[nm][s, C - 8:C, :], in_=QKV[s * 8:(s + 1) * 8, nq:nq + 2 * nk]),
                                  reads=[BQ])
                    S.op("pool", lambda e: e.tensor_copy(out=kb[:, 0:nk], in_=QKV[:, nq:nq + nk]), reads=[BQ], writes=[B_kb])
                    S.op("pool", lambda e: e.tensor_copy(out=v_all[:, slot, 0:nk], in_=QKV[:, nq + nk:nq + 2 * nk]),
                         reads=[BQ], writes=[B_v[slot]])
                    if full:
                        for cp in range(NCH):
                            src = QKV[:, cp * 2 * G * 64:(cp + 1) * 2 * G * 64].rearrange("p (e g d) -> p g e d", e=2, g=G)
                            dst = qb[:, cp * G * 128:(cp + 1) * G * 128].rearrange("p (g e d) -> p g e d", g=G, e=2)
                            S.op("pool", lambda e: e.tensor_copy(out=dst, in_=src), reads=[BQ], writes=[B_qb])

                def stage1b_late(ti, t):
                    slot = t["slot"]
                    full = t["kind"] != "halo"
                    for _ in pe_transposes([kb[:, cp * 128:(cp + 1) * 128] for cp in range(NCH)], [B_kb], tr_rr,
                                           lambda j0, n: kT_all[:, slot, j0:j0 + n, :], B_kT[slot]):
                        pass
                    if not full:
                        return
                    for _ in pe_transposes([qb[:, cq * 128:(cq + 1) * 128] for cq in range(NQC)], [B_qb], tr_rr,
                                           lambda j0, n: qT[ti % 2][:, j0:j0 + n, :], B_qT[ti % 2]):
                        pass

                def stage2(ti, t, par):
                    kind = t["kind"]
                    slot = t["slot"]
                    QT = qT[par]
                    BQT = B_qT[par]
                    started = set()

                    def pv(rhs_ap, v_ap, e2, out_fn, bank, reads, BPT):
                        for (ptile, Bp, lhs, rl) in ((p_ot, B_pot, v_ap, reads), (p_den, B_pden, ones[:, 0:64], [B_ones, BPT])):
                            key = (id(ptile), bank, e2)
                            first = key not in started
                            started.add(key)
                            S.op("pe", lambda e: e.matmul(out_fn(ptile), lhsT=lhs, rhs=rhs_ap, start=first, stop=True,
                                                          skip_group_check=True), reads=rl, writes=[Bp])

                    def qk_part(kslot, mask_off, hf):
                        kts = kT_all[:, kslot]
                        PT = PTs[pt_rr[0] % 2]
                        BPT = B_PTs[pt_rr[0] % 2]
                        pt_rr[0] += 1
                        if isA:
                            for e2 in range(2):
                                S.op("pe", lambda e: e.matmul(
                                    p_st[:, e2 * 512:(e2 + 1) * 512], lhsT=kts[64 * e2:64 * e2 + 64, 0, :],
                                    rhs=QT[64 * e2:64 * e2 + 64, hf * 4:(hf + 1) * 4, :], start=True, stop=True),
                                    reads=[B_kT[kslot], BQT], writes=[B_pst])
                        else:
                            for h in range(4):
                                cp, e2 = divmod(h, 2)
                                S.op("pe", lambda e: e.matmul(
                                    p_st[:, e2 * 512 + cp * 256:e2 * 512 + cp * 256 + 256], lhsT=kts[64 * e2:64 * e2 + 64, cp, :],
                                    rhs=QT[64 * e2:64 * e2 + 64, cp * 2:cp * 2 + 2, :], start=True, stop=True),
                                    reads=[B_kT[kslot], BQT], writes=[B_pst])
                        S.op("act", lambda e: e.activation(out=PT[:], in_=p_st[:], func=AF.Exp, scale=SCALE), reads=[B_pst], writes=[BPT])
                        PT3 = PT[:].rearrange("p (a b) -> p a b", a=8)
                        S.op("dve", lambda e: e.tensor_tensor(
                            out=PT3, in0=PT3, in1=maskt[:, mask_off:mask_off + 128].unsqueeze(1).to_broadcast([128, 8, 128]),
                            op=ALU.mult), reads=[BPT, B_mask], writes=[BPT])
                        return PT, BPT

                    def pv_part(kslot, hf, PT, BPT):
                        if isA:
                            for e2 in range(2):
                                pv(PT[:, e2 * 512:(e2 + 1) * 512], v_all[:, kslot, 64 * e2:64 * e2 + 64], e2,
                                   (lambda p, e2=e2, hf=hf: p[64 * e2:64 * e2 + 64, hf * 512:(hf + 1) * 512]), hf,
                                   [B_v[kslot], BPT], BPT)
                        else:
                            for h in range(4):
                                cp, e2 = divmod(h, 2)
                                pv(PT[:, e2 * 512 + cp * 256:e2 * 512 + cp * 256 + 256], v_all[:, kslot, 64 * h:64 * h + 64], e2,
                                   (lambda p, cp=cp, e2=e2: p[64 * e2:64 * e2 + 64, cp * 256:(cp + 1) * 256]), 0,
                                   [B_v[kslot], BPT], BPT)

                    def blocks_attn(blist):
                        subs = [(ks, mo, hf) for (ks, mo) in blist for hf in range(2 if isA else 1)]
                        prev_ = None
                        for sub in subs + [None]:
                            cur_ = None
                            if sub is not None:
                                PT, BPT = qk_part(*sub)
                                cur_ = (sub[0], sub[2], PT, BPT)
                                yield
                            if prev_ is not None:
                                pv_part(*prev_)
                                yield
                            prev_ = cur_

                    if kind == "main":
                        if isA:
                            moff = M_PA_H if t["prev_halo"] else M_PA
                        else:
                            moff = M_PB_H if t["prev_halo"] else M_PB
                        yield from blocks_attn([(slot, M_CUR), (t["prev"], moff)])
                    else:
                        yield from blocks_attn([(slot, {1: M_S1, 4: M_S4, 16: M_S16}[dil])])
                        nblk = {1: 1, 4: 4, 16: 8}[dil]
                        nt = {1: 8, 4: 2, 16: 1}[dil]
                        W = G * nt
                        half = nblk * NCH * W
                        def cache_load(s):
                            cbk, B_cbk = cbks[s % 2], B_cbks[s % 2]
                            if dil == 1:
                                S.dma(lambda e: e.dma_start(out=cbk[:, 0, 0:2 * nk], in_=cache[nm][s]), writes=[B_cbk], eng="pool")
                            elif dil == 4:
                                S.dma(lambda e: e.dma_start(out=cbk[:, 0:4, :], in_=cache[nm][s].rearrange("(m r) c -> m r c", r=4)),
                                      writes=[B_cbk], eng="pool")
                            else:
                                S.dma(lambda e: e.dma_start(out=cbk[:, 0:8, :],
                                                            in_=cache[nm][s].rearrange("(m r) c -> m r c", r=16)[:, 0:8, :]),
                                      writes=[B_cbk], eng="pool")
                        cache_load(0)
                        for s in range(NSAMP):
                            cbk = cbks[s % 2]
                            B_cbk = B_cbks[s % 2]
                            if s + 1 < NSAMP:
                                cache_load(s + 1)
                            items = [(b_, cp) for b_ in range(nblk) for cp in range(NCH)]
                            kTc_flat = kTc[:].rearrange("p a c b -> p (a c) b")
                            for _ in pe_transposes([cbk[:, b_, cp * 128:(cp + 1) * 128] for (b_, cp) in items], [B_cbk], tr_rr,
                                                   lambda j0, n: kTc_flat[:, j0:j0 + n, :], B_kTc):
                                pass

                            def tokc(b_):
                                if dil == 1:
                                    return slice(s * 8, s * 8 + 8)
                                if dil == 4:
                                    return slice(s * 8 + b_, s * 8 + b_ + 5, 4)
                                return slice(s * 8 + b_, s * 8 + b_ + 1)
                            PT = PTs[pt_rr[0] % 2]
                            BPT = B_PTs[pt_rr[0] % 2]
                            pt_rr[0] += 1
                            for b_ in range(nblk):
                                for h in range(Hkv):
                                    cp, e2 = divmod(h, 2)
                                    o0 = e2 * 512 + (b_ * NCH + cp) * W
                                    tk = tokc(b_)
                                    S.op("pe", lambda e: e.matmul(
                                        p_st[:, o0:o0 + W].rearrange("p (g t) -> p g t", g=G), lhsT=kTc_flat[64 * e2:64 * e2 + 64, b_ * NCH + cp, :],
                                        rhs=QT[64 * e2:64 * e2 + 64, cp * G:(cp + 1) * G, tk], start=True, stop=True),
                                        reads=[B_kTc, BQT], writes=[B_pst])
                            PTh = PT[:].rearrange("p (e c) -> p e c", e=2)[:, :, 0:half]
                            S.op("act", lambda e: e.activation(out=PTh, in_=p_st[:].rearrange("p (e c) -> p e c", e=2)[:, :, 0:half],
                                                               func=AF.Exp, scale=SCALE), reads=[B_pst], writes=[BPT])
                            if dil != 16:
                                moff = M_CA if isA else (M_CB0 if dil == 1 else M_CB1)
                                PTm = PTh.rearrange("p e (a t) -> p e a t", t=nt)
                                S.op("dve", lambda e: e.tensor_tensor(
                                    out=PTm, in0=PTm,
                                    in1=maskt[:, moff:moff + nt].unsqueeze(1).unsqueeze(1).to_broadcast([128, 2, half // nt, nt]),
                                    op=ALU.mult), reads=[BPT, B_mask], writes=[BPT])
                            yield
                            for b_ in range(nblk):
                                for h in range(Hkv):
                                    cp, e2 = divmod(h, 2)
                                    o0 = e2 * 512 + (b_ * NCH + cp) * W
                                    tk = tokc(b_)
                                    gs = 4 if isA else G
                                    for g0 in range(0, G, gs):
                                        c_lo_ = (cp * G + g0) * 128
                                        pv(PT[:, o0 + g0 * nt:o0 + (g0 + gs) * nt].rearrange("p (g t) -> p g t", g=gs),
                                           cbk[:, b_, nk + 64 * h:nk + 64 * h + 64], e2,
                                           (lambda p, e2=e2, tk=tk, c_lo_=c_lo_, gs=gs: p[64 * e2:64 * e2 + 64, c_lo_:c_lo_ + gs * 128]
                                            .rearrange("p (g q) -> p g q", g=gs)[:, :, tk]), c_lo_ // 512,
                                           [B_cbk, BPT], BPT)
                            yield

                    cols = sl(t["col0"], 128, t["cstep"])
                    if nm in ("B2", "B1"):
                        aN = accN[:, :, cols]
                        aD = accD[:, :, cols]
                        o3 = p_ot[:, 0:512].rearrange("p (a b) -> p a b", a=4)
                        d3 = p_den[:, 0:512].rearrange("p (a b) -> p a b", a=4)
                        if nm == "B2":
                            S.op("act", lambda e: e.activation(out=aN, in_=o3, func=AF.Copy), reads=[B_pot], writes=[B_accN])
                            S.op("dve", lambda e: e.tensor_copy(out=aD, in_=d3), reads=[B_pden], writes=[B_accD])
                        else:
                            S.op("dve", lambda e: e.tensor_tensor(out=aN, in0=o3, in1=aN, op=ALU.add), reads=[B_pot, B_accN], writes=[B_accN])
                            S.op("dve", lambda e: e.tensor_tensor(out=aD, in0=d3, in1=aD, op=ALU.add), reads=[B_pden, B_accD], writes=[B_accD])
                        return
                    n3 = nsum[:, 0:OW].rearrange("p (a b) -> p a b", a=NQC)
                    d3s = dsum[:, 0:OW].rearrange("p (a b) -> p a b", a=NQC)
                    o3 = p_ot[:, 0:OW].rearrange("p (a b) -> p a b", a=NQC)
                    d3 = p_den[:, 0:OW].rearrange("p (a b) -> p a b", a=NQC)
                    if isA:
                        S.op("dve", lambda e: e.tensor_tensor(out=d3s, in0=d3, in1=sinkexp[:].unsqueeze(2).to_broadcast([128, 8, 128]),
                                                              op=ALU.add), reads=[B_pden, B_sink], writes=[B_dsum])
                        S.op("act", lambda e: e.activation(out=nsum[:, 0:OW], in_=p_ot[:, 0:OW], func=AF.Copy), reads=[B_pot], writes=[B_nsum])
                    else:
                        S.op("dve", lambda e: e.tensor_tensor(out=d3s, in0=d3, in1=accD[:, :, cols], op=ALU.add),
                             reads=[B_pden, B_accD], writes=[B_dsum])
                        S.op("dve", lambda e: e.tensor_tensor(out=n3, in0=o3, in1=accN[:, :, cols], op=ALU.add),
                             reads=[B_pot, B_accN], writes=[B_nsum])
                    S.op("act", lambda e: e.activation(out=dsum[:, 0:OW], in_=dsum[:, 0:OW], func=AF.Ln), reads=[B_dsum], writes=[B_dsum])
                    S.op("act", lambda e: e.activation(out=dsum[:, 0:OW], in_=dsum[:, 0:OW], func=AF.Exp, scale=-1.0),
                         reads=[B_dsum], writes=[B_dsum])
                    S.op("dve", lambda e: e.tensor_tensor(out=nsum[:, 0:OW], in0=nsum[:, 0:OW], in1=dsum[:, 0:OW], op=ALU.mult),
                         reads=[B_nsum, B_dsum], writes=[B_nsum])
                    ytile, B_yt = ytiles[ti % 2], B_yts[ti % 2]
                    S.op("dve", lambda e: e.tensor_tensor(out=ytile[:, 0:NQC, :], in0=n3, in1=zTs[ti % 3][:, 0:NQC, :], op=ALU.mult),
                         reads=[B_nsum, B_zTs[ti % 3]], writes=[B_yt])
                    ydst = ya_s if isA else yb_s
                    S.dma(lambda e: e.dma_start(out=ydst[:, :, cols], in_=ytile[:, 0:NQC, :]), reads=[B_yt])
                    yield

                nT = len(tiles)
                x_load(tiles[0]["row0"], tiles[0]["step"], 0)
                if nT > 1:
                    x_load(tiles[1]["row0"], tiles[1]["step"], 1)
                for r in range(nT + 3):
                    if r + 2 < nT:
                        x_load(tiles[r + 2]["row0"], tiles[r + 2]["step"], (r + 2) % 3)
                    t0_, t1a, t1b, t2 = r, r - 1, r - 2, r - 3
                    g2 = None
                    if 0 <= t2 < nT and tiles[t2]["kind"] != "halo" and DBG["attn"]:
                        g2 = stage2(t2, tiles[t2], t2 % 2)
                    g1a = stage1a(t1a, tiles[t1a]) if 0 <= t1a < nT else None
                    if 0 <= t1b < nT:
                        stage1b_early(t1b, tiles[t1b])
                    if g2 is not None:
                        try:
                            next(g2)
                        except StopIteration:
                            g2 = None
                    g0 = stage0(t0_, tiles[t0_]) if 0 <= t0_ < nT else None
                    run_interleaved([g1a, g2, g0])
                    if 0 <= t1a < nT:
                        for _ in stage1z(t1a, tiles[t1a]):
                            pass
                    if t1a == nT - 1:
                        load_pass_weights(gi + 1)
                    if 0 <= t1b < nT:
                        stage1b_late(t1b, tiles[t1b])

        bgroups = [g for g in GROUPS if g["name"] != "A" and (DBG["groups"] is None or g["name"] in DBG["groups"])]
        agroups = [g for g in GROUPS if g["name"] == "A" and (DBG["groups"] is None or g["name"] in DBG["groups"])]
        with ExitStack() as ph:
            attention_phase(bgroups, ph, False)
        S.barrier()
        with ExitStack() as ph:
            attention_phase(agroups, ph, True)
        issue_bg(len(bg_pending))
        issue_wconv(len(wconv_pending))
        S.barrier()

        with ExitStack() as ph:
            def psum(name, shp, dt=F32):
                uniq[0] += 1
                return ph.enter_context(nc.psum_tensor(f"{name}_{uniq[0]}", shp, dt)), Buf()
            p_ga, B_pga = psum("p_ga", [128, 512])
            p_gb, B_pgb = psum("p_gb", [128, 512])
            p_ma, B_pma = psum("p_ma", [128, 512])
            p_mb, B_pmb = psum("p_mb", [128, 512])
            acc_rr = RR([psum("p_acc0", [128, 512]), psum("p_acc1", [128, 512])])
            tr_rr = RR([psum("p_tr0", [128, 512]), psum("p_tr1", [128, 512])])
            NH = 768
            mTh = sb(ph, "mTh", [128, 16, NH], BF16); B_mTh = Buf()
            lng = sb(ph, "lng", [128, DM], F32); B_lng = Buf()
            lnb = sb(ph, "lnb", [128, DM], F32); B_lnb = Buf()
            S.dma(lambda e: e.dma_start(out=lng[:], in_=ln_g[0, :].partition_broadcast(128)), writes=[B_lng])
            S.dma(lambda e: e.dma_start(out=lnb[:], in_=ln_b[0, :].partition_broadcast(128)), writes=[B_lnb])

            halves = [(0, 6), (6, 12), (12, 17)] if DBG["out"] else []
            for hi, (t0, t1) in enumerate(halves):
                nt_h = t1 - t0
                ntok = 128 * nt_h
                tok0 = 128 * t0
                with ExitStack() as s1:
                    xTh = sb(s1, "xTh", [128, 16, NH], BF16); B_xTh = Buf()
                    yaq = sb(s1, "yaq", [128, 8, NH], BF16); B_yaq = Buf()
                    ybq = sb(s1, "ybq", [128, 4, NH], BF16); B_ybq = Buf()
                    wch = [sb(s1, f"wch{i}", [128, WCH], BF16) for i in range(2)]; B_wch = [Buf(), Buf()]
                    sa = [sb(s1, f"sa{i}", [128, 512], F32) for i in range(2)]; B_sa = [Buf(), Buf()]
                    sbg = [sb(s1, f"sbg{i}", [128, 512], F32) for i in range(2)]; B_sbg = [Buf(), Buf()]
                    tA = [sb(s1, f"tA{i}", [128, 512], F32) for i in range(2)]; B_tA = [Buf(), Buf()]
                    tB = [sb(s1, f"tB{i}", [128, 512], F32) for i in range(2)]; B_tB = [Buf(), Buf()]
                    S.dma(lambda e: e.dma_start(out=yaq[:, :, 0:ntok], in_=ya_s[:, :, tok0:tok0 + ntok]), writes=[B_yaq])
                    S.dma(lambda e: e.dma_start(out=ybq[:, :, 0:ntok], in_=yb_s[:, :, tok0:tok0 + ntok]), writes=[B_ybq])
                    for tl in range(min(2, nt_h)):
                        x_load(2048 + 128 * (t0 + tl), 1, tl % 3)
                    for tl in range(nt_h):
                        if tl + 2 < nt_h:
                            x_load(2048 + 128 * (t0 + tl + 2), 1, (tl + 2) % 3)
                        for _ in x_transpose(tl % 3, xTh[:, :, tl * 128:(tl + 1) * 128], B_xTh, tr_rr):
                            pass
                    tgs = [(n0, min(n0 + 512, ntok)) for n0 in range(0, ntok, 512)]
                    it = 0
                    def wviews(cg2):
                        w_ = wch[cg2 % 2]
                        return (w_, B_wch[cg2 % 2],
                                w_[:, 0:4096].rearrange("p (k n) -> p k n", k=16),
                                w_[:, 4096:8192].rearrange("p (k n) -> p k n", k=16),
                                w_[:, 8192:10240].rearrange("p (k n) -> p k n", k=8),
                                w_[:, 10240:11264].rearrange("p (k n) -> p k n", k=4))

                    def wload(cg2):
                        w_, Bw, wga2, wgb2, wba2, wbb2 = wviews(cg2)
                        S.dma(lambda e: e.dma_start(out=w_[:], in_=wsc[cg2]), reads=[B_wsc[cg2]], writes=[Bw])

                    wload(0)
                    for c in range(16):
                        cg2, ci = divmod(c, 2)
                        w_, Bw, wga2, wgb2, wba2, wbb2 = wviews(cg2)
                        wga = wga2[:, :, ci * 128:(ci + 1) * 128]
                        wgb = wgb2[:, :, ci * 128:(ci + 1) * 128]
                        wba = wba2[:, :, ci * 128:(ci + 1) * 128]
                        wbb = wbb2[:, :, ci * 128:(ci + 1) * 128]
                        if ci == 0 and cg2 + 1 < 8:
                            wload(cg2 + 1)
                        for (n0, n1) in tgs:
                            n = n1 - n0
                            j = it % 2
                            it += 1
                            for k in range(16):
                                S.op("pe", lambda e: e.matmul(p_ga[:, 0:n], lhsT=wga[:, k, :], rhs=xTh[:, k, n0:n1],
                                                              start=(k == 0), stop=(k == 15)), reads=[Bw, B_xTh], writes=[B_pga])
                            S.op("act", lambda e: e.activation(out=sa[j][:, 0:n], in_=p_ga[:, 0:n], func=AF.Sigmoid,
                                                               bias=bfm[:, 12 + c:13 + c]), reads=[B_pga, B_bfm], writes=[B_sa[j]])
                            for k in range(16):
                                S.op("pe", lambda e: e.matmul(p_gb[:, 0:n], lhsT=wgb[:, k, :], rhs=xTh[:, k, n0:n1],
                                                              start=(k == 0), stop=(k == 15)), reads=[Bw, B_xTh], writes=[B_pgb])
                            S.op("act", lambda e: e.activation(out=sbg[j][:, 0:n], in_=p_gb[:, 0:n], func=AF.Sigmoid,
                                                               bias=bfm[:, 28 + c:29 + c]), reads=[B_pgb, B_bfm], writes=[B_sbg[j]])
                            for k in range(8):
                                S.op("pe", lambda e: e.matmul(p_ma[:, 0:n], lhsT=wba[:, k, :], rhs=yaq[:, k, n0:n1],
                                                              start=(k == 0), stop=(k == 7)), reads=[Bw, B_yaq], writes=[B_pma])
                            for k in range(4):
                                S.op("pe", lambda e: e.matmul(p_mb[:, 0:n], lhsT=wbb[:, k, :], rhs=ybq[:, k, n0:n1],
                                                              start=(k == 0), stop=(k == 3)), reads=[Bw, B_ybq], writes=[B_pmb])
                            S.op("dve", lambda e: e.tensor_tensor(out=tA[j][:, 0:n], in0=p_ma[:, 0:n], in1=sa[j][:, 0:n], op=ALU.mult),
                                 reads=[B_pma, B_sa[j]], writes=[B_tA[j]])
                            S.op("dve", lambda e: e.tensor_tensor(out=tB[j][:, 0:n], in0=p_mb[:, 0:n], in1=sbg[j][:, 0:n], op=ALU.mult),
                                 reads=[B_pmb, B_sbg[j]], writes=[B_tB[j]])
                            S.op("pool", lambda e: e.tensor_tensor(out=mTh[:, c, n0:n1], in0=tA[j][:, 0:n], in1=tB[j][:, 0:n], op=ALU.add),
                                 reads=[B_tA[j], B_tB[j]], writes=[B_mTh])
                S.barrier()
                with ExitStack() as s2:
                    wo = sb(s2, "wo", [128, 16, DM], BF16); B_wo = [Buf() for _ in range(4)]
                    hs = [sb(s2, f"hs{i}", [128, DM], F32) for i in range(2)]; B_hs = [Buf(), Buf()]
                    hn = [sb(s2, f"hn{i}", [128, DM], F32) for i in range(2)]; B_hn = [Buf(), Buf()]
                    stats = sb(s2, "stats", [128, 4, 6], F32); B_stats = Buf()
                    mv = sb(s2, "mv", [128, 4], F32); B_mv = Buf()
                    for cg in range(4):
                        S.dma(lambda e: e.dma_start(out=wo[:, :, cg * 512:(cg + 1) * 512], in_=wosc[:, :, cg * 512:(cg + 1) * 512]),
                              reads=[B_wosc[cg]], writes=[B_wo[cg]])
                    mo = [sb(s2, f"mo{i}", [128, DM], F32) for i in range(2)]; B_mo = [Buf(), Buf()]

                    def F1(tl):
                        j = tl % 2
                        row0 = 2048 + 128 * (t0 + tl)
                        S.dma(lambda e: e.dma_start(out=hs[j][:], in_=xe[row0:row0 + 128, :]), writes=[B_hs[j]])
                        for cg in range(4):
                            pacc, Bpa = acc_rr.next()
                            for k in range(16):
                                S.op("pe", lambda e: e.matmul(pacc[:], lhsT=mTh[:, k, tl * 128:(tl + 1) * 128], rhs=wo[:, k, cg * 512:(cg + 1) * 512],
                                                              start=(k == 0), stop=(k == 15)), reads=[B_mTh, B_wo[cg]], writes=[Bpa])
                            S.op("act", lambda e: e.activation(out=mo[j][:, cg * 512:(cg + 1) * 512], in_=pacc[:], func=AF.Copy),
                                 reads=[Bpa], writes=[B_mo[j]])

                    def F2(tl):
                        j = tl % 2
                        S.op("dve", lambda e: e.scalar_tensor_tensor(out=hs[j][:], in0=hs[j][:], scalar=ALPHA, in1=mo[j][:],
                                                                    op0=ALU.mult, op1=ALU.add), reads=[B_hs[j], B_mo[j]], writes=[B_hs[j]])
                        for i4 in range(4):
                            S.op("dve", lambda e: e.bn_stats(out=stats[:, i4, :], in_=hs[j][:, i4 * 512:(i4 + 1) * 512]),
                                 reads=[B_hs[j]], writes=[B_stats])
                        S.op("dve", lambda e: e.bn_aggr(out=mv[:, 0:2], in_=stats[:].rearrange("p a b -> p (a b)")), reads=[B_stats], writes=[B_mv])
                        S.op("act", lambda e: e.activation(out=mv[:, 2:3], in_=mv[:, 1:2], func=AF.Sqrt, bias=LN_EPS), reads=[B_mv], writes=[B_mv])
                        S.op("dve", lambda e: e.reciprocal(out=mv[:, 2:3], in_=mv[:, 2:3]), reads=[B_mv], writes=[B_mv])
                        S.op("dve", lambda e: e.tensor_tensor(out=mv[:, 3:4], in0=mv[:, 0:1], in1=mv[:, 2:3], op=ALU.mult), reads=[B_mv], writes=[B_mv])
                        S.op("dve", lambda e: e.tensor_scalar(out=mv[:, 3:4], in0=mv[:, 3:4], scalar1=-1.0, scalar2=None, op0=ALU.mult),
                             reads=[B_mv], writes=[B_mv])
                        S.op("pool", lambda e: e.tensor_scalar(out=hn[j][:], in0=hs[j][:], scalar1=mv[:, 2:3], scalar2=mv[:, 3:4],
                                                              op0=ALU.mult, op1=ALU.add), reads=[B_hs[j], B_mv], writes=[B_hn[j]])
                        S.op("dve", lambda e: e.tensor_tensor(out=hn[j][:], in0=hn[j][:], in1=lng[:], op=ALU.mult), reads=[B_hn[j], B_lng], writes=[B_hn[j]])
                        S.op("pool", lambda e: e.tensor_tensor(out=hn[j][:], in0=hn[j][:], in1=lnb[:], op=ALU.add), reads=[B_hn[j], B_lnb], writes=[B_hn[j]])
                        r0 = 128 * (t0 + tl)
                        S.dma(lambda e: e.dma_start(out=y_d[r0:r0 + 128, :], in_=hn[j][:]), reads=[B_hn[j]])

                    F1(0)
                    for tl in range(nt_h):
                        if tl + 1 < nt_h:
                            F1(tl + 1)
                        F2(tl)
                S.barrier()
        S.barrier()
        S.emit()
    return nc


def _masks(halo_valid):
    k = np.arange(128)[:, None]
    q = np.arange(128)[None, :]
    m = np.zeros((128, NMASK), np.float32)
    m[:, M_CUR:M_CUR + 128] = (k <= q)
    m[:, M_PA:M_PA + 128] = (k > q)
    m[:, M_PB:M_PB + 128] = (k >= q)
    m[:, M_PA_H:M_PA_H + 128] = (k > q) * halo_valid
    m[:, M_PB_H:M_PB_H + 128] = (k >= q) * halo_valid
    ss, ts = k // 8, k % 8
    sq, tq = q // 8, q % 8
    same = (ss == sq) & (ts <= tq)
    m[:, M_S1:M_S1 + 128] = same
    m[:, M_S4:M_S4 + 128] = same & ((tq - ts) % 4 == 0)
    m[:, M_S16:M_S16 + 128] = same & (tq == ts)
    t8 = np.arange(8)[None, :]
    m[:, M_CA:M_CA + 8] = (k >= t8 + 1)
    m[:, M_CB0:M_CB0 + 8] = (k >= t8)
    m[:, M_CB1] = 1.0
    m[:, M_CB1 + 1] = (k[:, 0] >= 1)
    m[:, M_ID:M_ID + 128] = np.eye(128)
    return m


def _rope_tables(start):
    inv = np.power(np.float32(THETA), -(np.arange(0, 16, 2, dtype=np.float32) / np.float32(16))).astype(np.float32)
    cs = np.zeros((4, 128, 33, 16), np.float32)
    p = np.arange(128)
    for gi, g in enumerate(GROUPS):
        for ti, t in enumerate(tile_list(g["order"])):
            if t["kind"] == "samp":
                pos = PAST + (p % 8)
            else:
                pos = (start - 2048) + t["row0"] + t["step"] * p
            ang = pos.astype(np.float32)[:, None] * inv[None, :]
            cs[gi, :, ti, 0:8] = np.cos(ang)
            cs[gi, :, ti, 8:16] = np.sin(ang)
    return cs


_NC_CACHE = {}


def kernel(x_prompt, x_sample, cache_a_kv, cache_b0_kv, cache_b1_kv, cache_b2_kv,
           w_in, b_in, sink_a, w_br_a, w_br_b, w_out, ln_g, ln_b):
    f32 = np.float32
    x_prompt = np.asarray(x_prompt, f32); x_sample = np.asarray(x_sample, f32)
    w_in2 = np.ascontiguousarray(np.asarray(w_in, f32)[0])
    b_in2 = np.ascontiguousarray(np.asarray(b_in, f32))
    bflat = b_in2[0]
    p = np.arange(128)
    e_, d_ = p // 64, p % 64
    bfm = np.zeros((128, 44), f32)
    for g in range(8):
        bfm[:, g] = bflat[ZA0 + e_ * 512 + g * 64 + d_]
    for cq in range(4):
        cp, gg = divmod(cq, 2)
        bfm[:, 8 + cq] = bflat[ZB0 + (2 * cp + e_) * 128 + gg * 64 + d_]
    for c in range(16):
        bfm[:, 12 + c] = bflat[GA0 + 128 * c + p]
        bfm[:, 28 + c] = bflat[GB0 + 128 * c + p]
    sk = np.asarray(sink_a, f32)[0]
    sinkl = np.zeros((128, 8), f32)
    for g in range(8):
        sinkl[:, g] = sk[e_ * 8 + g]
    caches = {"cache_a": np.asarray(cache_a_kv, f32)[0].reshape(128, 128, 256),
              "cache_b0": np.asarray(cache_b0_kv, f32)[0].reshape(128, 128, 512),
              "cache_b1": np.asarray(cache_b1_kv, f32)[0].reshape(128, 512, 512),
              "cache_b2": np.asarray(cache_b2_kv, f32)[0].reshape(128, 2048, 512)}
    bzrow = np.ascontiguousarray(bfm[:, 0:12].T.reshape(1, 1536))
    common = {"w_in": w_in2, "b_in": b_in2, "bfm": bfm, "bzrow": bzrow, "sinkl": sinkl,
              "w_br_a": np.ascontiguousarray(np.asarray(w_br_a, f32)[0]),
              "w_br_b": np.ascontiguousarray(np.asarray(w_br_b, f32)[0]),
              "w_out": np.ascontiguousarray(np.asarray(w_out, f32)[0]),
              "ln_g": np.ascontiguousarray(np.asarray(ln_g, f32)), "ln_b": np.ascontiguousarray(np.asarray(ln_b, f32))}
    in_maps = []
    for c in range(NCORE):
        n, j = divmod(c, 4)
        start = CH * j
        xe = np.zeros((4224, DM), f32)
        if j > 0:
            xe[0:2048] = x_prompt[n, start - 2048:start]
        xe[2048:4096] = x_prompt[n, start:start + CH]
        xe[4096:] = x_sample[NSAMP * c:NSAMP * (c + 1)].reshape(128, DM)
        m = dict(common)
        m["xe"] = xe
        m["cs"] = _rope_tables(start)
        m["masks"] = _masks(1.0 if j > 0 else 0.0)
        for k, v in caches.items():
            m[k] = np.ascontiguousarray(v[NSAMP * c:NSAMP * (c + 1)])
        in_maps.append(m)

    if "nc" not in _NC_CACHE:
        _NC_CACHE["nc"] = build_program()
    res = run_bass_kernel_spmd(_NC_CACHE["nc"], in_maps, core_ids=list(range(NCORE)))
    R = res.results

    y_prompt = np.zeros((2, 8192, DM), f32)
    y_sample = np.zeros((128, 8, DM), f32)
    for c in range(NCORE):
        n, j = divmod(c, 4)
        y_prompt[n, CH * j:CH * (j + 1)] = R[c]["y"][0:CH]
        y_sample[NSAMP * c:NSAMP * (c + 1)] = R[c]["y"][CH:].reshape(NSAMP, 8, DM)
    shp = {"a": (2, 2, 64), "b0": (2, 4, 64), "b1": (2, 4, 64), "b2": (2, 4, 64)}
    rows = {"a": 128, "b0": 128, "b1": 512, "b2": 2048}
    pouts, souts = [], []
    for key in ("a", "b0", "b1", "b2"):
        po = np.stack([R[4 * n + 3]["pst_" + key].reshape((rows[key],) + shp[key]) for n in range(2)], 0)[None]
        pouts.append(np.ascontiguousarray(po.astype(f32)))
        so = np.concatenate([R[c]["sst_" + key].reshape((NSAMP, rows[key]) + shp[key]) for c in range(NCORE)], 0)[None]
        souts.append(np.ascontiguousarray(so.astype(f32)))
    return (y_prompt, y_sample, pouts[0], pouts[1], pouts[2], pouts[3], souts[0], souts[1], souts[2], souts[3])
```

```python
import numpy as np
from contextlib import ExitStack
import concourse.bass as bass
import concourse.mybir as mybir
from concourse.bass_utils import run_bass_kernel_spmd

F32 = mybir.dt.float32
BF16 = mybir.dt.bfloat16
AF = mybir.ActivationFunctionType
ALU = mybir.AluOpType

NCORE = 8
DM = 2048
CH = 2048
NSAMP = 16
PAST = 16384
NTOK = CH + 128
ALPHA = 2.0 ** 0.25
LN_EPS = 1e-5
SCALE = 0.125
THETA = 500000.0

GROUPS = [
    dict(name="B2", col0=4352, nq=512, nk=256, Hkv=4, G=2, dil=16, C=2048, order="f16", pst_rows=2048),
    dict(name="B1", col0=3328, nq=512, nk=256, Hkv=4, G=2, dil=4, C=512, order="f4", pst_rows=512),
    dict(name="B0", col0=2304, nq=512, nk=256, Hkv=4, G=2, dil=1, C=128, order="nat", pst_rows=128),
    dict(name="A", col0=0, nq=1024, nk=128, Hkv=2, G=8, dil=1, C=128, order="nat", pst_rows=128),
]
ZA0, ZB0, GA0, GB0 = 1280, 5376, 5888, 7936

DBG = {"groups": None, "tiles": None, "out": True, "bgcopy": True, "samp": True, "attn": True}

M_CUR, M_PA_H, M_PB_H, M_PA, M_PB, M_S1, M_S4, M_S16 = 0, 128, 256, 384, 512, 640, 768, 896
M_CA, M_CB0, M_CB1, M_ID = 1024, 1032, 1040, 1056
NMASK = 1184


def sl(start, n, step):
    return slice(start, start + (n - 1) * step + 1, step)


def tile_list(order):
    tl = []
    if order == "nat":
        tl.append(dict(kind="halo", row0=2048 - 128, step=1, slot=0))
        for b in range(16):
            tl.append(dict(kind="main", row0=2048 + 128 * b, step=1, slot=1 + b % 3,
                           prev=(0 if b == 0 else 1 + (b - 1) % 3), prev_halo=(b == 0), col0=128 * b, cstep=1))
    elif order == "f4":
        for r in range(4):
            tl.append(dict(kind="halo", row0=2048 - 512 + r, step=4, slot=r))
        for t in range(16):
            M, r = divmod(t, 4)
            tl.append(dict(kind="main", row0=2048 + 512 * M + r, step=4, slot=4 + t % 8,
                           prev=(r if M == 0 else 4 + (t - 4) % 8), prev_halo=(M == 0), col0=512 * M + r, cstep=4))
    else:
        for r in range(16):
            tl.append(dict(kind="halo", row0=r, step=16, slot=r))
        for r in range(16):
            tl.append(dict(kind="main", row0=2048 + r, step=16, slot=16 + r % 2, prev=r, prev_halo=True, col0=r, cstep=16))
    tl.append(dict(kind="samp", row0=4096, step=1, slot={"nat": 2, "f4": 4, "f16": 16}[order], col0=2048, cstep=1))
    return tl


class Buf:
    __slots__ = ("last_w", "readers")

    def __init__(self):
        self.last_w = None
        self.readers = {}


class _Rec:
    def __getattr__(self, name):
        def f(*a, **kw):
            return (name, a, kw)
        return f


_REC = _Rec()


class Sched:
    ENGS = ("pe", "act", "dve", "pool", "sp")

    def __init__(self, nc, stack, n_dma_sems=90):
        self.nc = nc
        self.sems = {}
        for k in self.ENGS:
            self.sems[k] = stack.enter_context(nc.semaphore(f"sem_{k}"))
        self.n_dma = n_dma_sems
        for j in range(n_dma_sems):
            self.sems[f"d{j}"] = stack.enter_context(nc.semaphore(f"sem_d{j}"))
        self.sems["bg"] = stack.enter_context(nc.semaphore("sem_bg"))
        self.bg_val = 0
        self.dma_val = [0] * n_dma_sems
        self.dma_pool = {"sp": list(range(0, 44)), "pool": list(range(44, 86)), "act": list(range(86, n_dma_sems))}
        self.dma_rr = {"sp": 0, "pool": 0, "act": 0}
        self.cnt = {k: 0 for k in self.ENGS}
        self.prog = {k: [] for k in self.ENGS}
        self.waited = {k: {} for k in self.ENGS}

    def _deps(self, reads, writes):
        deps = {}

        def add(k, v):
            if deps.get(k, 0) < v:
                deps[k] = v
        for b in reads:
            if b.last_w is not None:
                add(*b.last_w)
        for b in writes:
            if b.last_w is not None:
                add(*b.last_w)
            for k, v in b.readers.items():
                add(k, v)
        return deps

    def _commit(self, tok, reads, writes):
        k, v = tok
        for b in reads:
            if b.readers.get(k, 0) < v:
                b.readers[k] = v
        for b in writes:
            b.last_w = tok
            b.readers = {}

    def _waits(self, eng, deps, skip_self_pe=True):
        waits = []
        w = self.waited[eng]
        for k, v in deps.items():
            if k == eng and eng == "pe" and skip_self_pe:
                continue
            if w.get(k, 0) >= v:
                continue
            w[k] = v
            waits.append((k, v))
        return waits

    def op(self, eng, fn, reads=(), writes=()):
        deps = self._deps(reads, writes)
        waits = self._waits(eng, deps)
        self.cnt[eng] += 1
        tok = (eng, self.cnt[eng])
        self.prog[eng].append((waits, fn(_REC), tok, 1))
        self._commit(tok, reads, writes)
        return tok

    def dma_bg(self, fn, eng="act"):
        self.bg_val += 16
        self.prog[eng].append(([], fn(_REC), ("bg", self.bg_val), 16))

    def dma(self, fn, reads=(), writes=(), eng="sp"):
        deps = self._deps(reads, writes)
        pl = self.dma_pool[eng]
        j = pl[self.dma_rr[eng] % len(pl)]
        self.dma_rr[eng] += 1
        key = f"d{j}"
        if self.dma_val[j] > 0 and deps.get(key, 0) < self.dma_val[j]:
            deps[key] = self.dma_val[j]
        waits = self._waits(eng, deps, skip_self_pe=False)
        self.dma_val[j] += 16
        tok = (key, self.dma_val[j])
        self.prog[eng].append((waits, fn(_REC), tok, 16))
        self._commit(tok, reads, writes)
        return tok

    def barrier(self):
        allw = [(k, self.cnt[k]) for k in self.ENGS if self.cnt[k] > 0]
        allw += [(f"d{j}", self.dma_val[j]) for j in range(self.n_dma) if self.dma_val[j] > 0]
        if self.bg_val > 0:
            allw.append(("bg", self.bg_val))
        for eng in self.ENGS:
            waits = []
            w = self.waited[eng]
            for k, v in allw:
                if k == eng:
                    continue
                if w.get(k, 0) >= v:
                    continue
                w[k] = v
                waits.append((k, v))
            self.prog[eng].append((waits, None, None, 0))

    def emit(self):
        nc, sems, prog = self.nc, self.sems, self.prog

        def run(e, items):
            for waits, fn, tok, inc in items:
                for k, v in waits:
                    e.wait_ge(sems[k], v)
                if fn is None:
                    continue
                ins = getattr(e, fn[0])(*fn[1], **fn[2])
                ins.then_inc(sems[tok[0]], inc)

        with nc.Block() as block:
            @block.sync
            def _(e):
                run(e, prog["sp"])

            @block.tensor
            def _(e):
                run(e, prog["pe"])

            @block.scalar
            def _(e):
                run(e, prog["act"])

            @block.vector
            def _(e):
                run(e, prog["dve"])

            @block.gpsimd
            def _(e):
                run(e, prog["pool"])


def build_program():
    nc = bass.Bass("TRN2", target_bir_lowering=False)

    def din(name, shape, dt=F32):
        return nc.dram_tensor(name, shape, dt, kind="ExternalInput").ap()

    def dout(name, shape, dt=F32):
        return nc.dram_tensor(name, shape, dt, kind="ExternalOutput").ap()

    xe = din("xe", [4224, DM])
    w_in = din("w_in", [DM, 9984])
    b_in = din("b_in", [1, 9984])
    bfm_d = din("bfm", [128, 44])
    bzrow_d = din("bzrow", [1, 1536])
    sink_d = din("sinkl", [128, 8])
    w_br_a = din("w_br_a", [1024, DM])
    w_br_b = din("w_br_b", [512, DM])
    w_out = din("w_out", [DM, DM])
    ln_g = din("ln_g", [1, DM])
    ln_b = din("ln_b", [1, DM])
    cs_d = din("cs", [4, 128, 33, 16])
    masks_d = din("masks", [128, NMASK])
    cache = {"A": din("cache_a", [NSAMP, 128, 256]), "B0": din("cache_b0", [NSAMP, 128, 512]),
             "B1": din("cache_b1", [NSAMP, 512, 512]), "B2": din("cache_b2", [NSAMP, 2048, 512])}
    y_d = dout("y", [NTOK, DM])
    pst = {"A": dout("pst_a", [128, 256]), "B0": dout("pst_b0", [128, 512]),
           "B1": dout("pst_b1", [512, 512]), "B2": dout("pst_b2", [2048, 512])}
    sst = {"A": dout("sst_a", [NSAMP, 128, 256]), "B0": dout("sst_b0", [NSAMP, 128, 512]),
           "B1": dout("sst_b1", [NSAMP, 512, 512]), "B2": dout("sst_b2", [NSAMP, 2048, 512])}
    ya_s = nc.dram_tensor("ya_s", [128, 8, NTOK], BF16, kind="Internal").ap()
    yb_s = nc.dram_tensor("yb_s", [128, 4, NTOK], BF16, kind="Internal").ap()

    with ExitStack() as st:
        S = Sched(nc, st)

        uniq = [0]

        def sb(stack, name, shp, dt):
            uniq[0] += 1
            return stack.enter_context(nc.sbuf_tensor(f"{name}_{uniq[0]}", shp, dt))

        maskt = sb(st, "maskt", [128, NMASK], BF16); B_mask = Buf()
        ones = sb(st, "ones", [128, 128], BF16); B_ones = Buf()
        bzr = sb(st, "bzr", [128, 1536], BF16); B_bzr = Buf()
        bfm = sb(st, "bfm_sb", [128, 44], F32); B_bfm = Buf()
        xb = [sb(st, f"xb{i}", [128, DM], BF16) for i in range(3)]; B_xb = [Buf(), Buf(), Buf()]
        ident = maskt[:, M_ID:M_ID + 128]

        S.dma(lambda e: e.dma_start(out=maskt[:], in_=masks_d), writes=[B_mask], eng="pool")
        S.dma(lambda e: e.dma_start(out=bfm[:], in_=bfm_d), writes=[B_bfm])
        S.op("pool", lambda e: e.memset(bzr[:], 0.0), writes=[B_bzr])
        S.dma(lambda e: e.dma_start(out=bzr[0:1, :], in_=bzrow_d), writes=[B_bzr], eng="pool")
        S.op("dve", lambda e: e.memset(ones[:], 1.0), writes=[B_ones])

        bg_pending = []
        for g in (GROUPS if DBG["bgcopy"] else []):
            nm, C = g["name"], g["C"]
            for s in range(NSAMP):
                bg_pending.append((nm, C, s))

        def issue_bg(n=1):
            for _ in range(n):
                if bg_pending:
                    nm, C, s = bg_pending.pop(0)
                    S.dma_bg(lambda e: e.dma_start(out=sst[nm][s, 0:C - 8, :], in_=cache[nm][s, 8:C, :]))

        WCH = 2 * (2048 + 2048 + 1024 + 512)
        wsc = nc.dram_tensor("wsc", [8, 128, WCH], BF16, kind="Internal").ap()
        B_wsc = [Buf() for _ in range(8)]
        wosc = nc.dram_tensor("wosc", [128, 16, DM], BF16, kind="Internal").ap()
        B_wosc = [Buf() for _ in range(4)]
        wconv_pending = []
        for cg2 in range(8):
            c0_ = 256 * cg2
            dst = wsc[cg2]
            wconv_pending.append((cg2, dst[:, 0:4096].rearrange("p (k n) -> p k n", k=16),
                                  w_in[:, GA0 + c0_:GA0 + c0_ + 256].rearrange("(k p) n -> p k n", p=128)))
            wconv_pending.append((cg2, dst[:, 4096:8192].rearrange("p (k n) -> p k n", k=16),
                                  w_in[:, GB0 + c0_:GB0 + c0_ + 256].rearrange("(k p) n -> p k n", p=128)))
            for e2 in range(2):
                wconv_pending.append((cg2, dst[64 * e2:64 * e2 + 64, 8192:10240].rearrange("p (k n) -> p k n", k=8),
                                      w_br_a[e2 * 512:(e2 + 1) * 512, c0_:c0_ + 256].rearrange("(g d) n -> d g n", d=64)))
                for cp in range(2):
                    wconv_pending.append((cg2, dst[64 * e2:64 * e2 + 64, 10240:11264].rearrange("p (k n) -> p k n", k=4)[:, cp * 2:cp * 2 + 2, :],
                                          w_br_b[(2 * cp + e2) * 128:(2 * cp + e2 + 1) * 128, c0_:c0_ + 256]
                                          .rearrange("(g d) n -> d g n", d=64)))

        for cg in range(4):
            wconv_pending.append((("o", cg), wosc[:, :, cg * 512:(cg + 1) * 512],
                                  w_out[:, cg * 512:(cg + 1) * 512].rearrange("(k p) n -> p k n", p=128)))

        def issue_wconv(n):
            for _ in range(n):
                if wconv_pending:
                    cg2, o_, i_ = wconv_pending.pop(0)
                    Bw_ = B_wosc[cg2[1]] if isinstance(cg2, tuple) else B_wsc[cg2]
                    S.dma(lambda e: e.dma_start(out=o_, in_=i_), writes=[Bw_], eng="pool")

        evac_flip = [0]

        def evac_copy(out_ap, in_ap, reads, writes):
            evac_flip[0] ^= 1
            if evac_flip[0]:
                S.op("act", lambda e: e.activation(out=out_ap, in_=in_ap, func=AF.Copy), reads=reads, writes=writes)
            else:
                S.op("dve", lambda e: e.tensor_copy(out=out_ap, in_=in_ap), reads=reads, writes=writes)

        class RR:
            def __init__(self, items):
                self.items = items
                self.i = 0

            def next(self):
                it = self.items[self.i % len(self.items)]
                self.i += 1
                return it

        def x_load(row0, step, i):
            S.dma(lambda e: e.dma_start(out=xb[i][:], in_=xe[sl(row0, 128, step), :]), writes=[B_xb[i]], eng="pool")

        def pe_transposes(srcs, reads, trr, dst_fn, dst_buf):
            for j0 in range(0, len(srcs), 4):
                grp = srcs[j0:j0 + 4]
                ptr, Bp = trr.next()
                for j, src in enumerate(grp):
                    S.op("pe", lambda e: e.matmul(ptr[:, j * 128:(j + 1) * 128], lhsT=src, rhs=ident, start=True, stop=True),
                         reads=reads + [B_mask], writes=[Bp])
                evac_copy(dst_fn(j0, len(grp)), ptr[:, 0:len(grp) * 128].rearrange("p (a b) -> p a b", a=len(grp)), [Bp], [dst_buf])
                yield

        def x_transpose(i, dst, dst_buf, trr):
            yield from pe_transposes([xb[i][:, k * 128:(k + 1) * 128] for k in range(16)], [B_xb[i]], trr,
                                     lambda j0, n: dst[:, j0:j0 + n, :], dst_buf)

        def run_interleaved(gens):
            alive = [g for g in gens if g is not None]
            while alive:
                for g in list(alive):
                    try:
                        next(g)
                    except StopIteration:
                        alive.remove(g)

        def attention_phase(groups, ph, phaseA):
            def psum(name, shp, dt=F32):
                uniq[0] += 1
                return ph.enter_context(nc.psum_tensor(f"{name}_{uniq[0]}", shp, dt)), Buf()
            p_st, B_pst = psum("p_st", [128, 1024])
            if phaseA:
                p_ot, B_pot = psum("p_ot", [128, 1024])
                p_den, B_pden = psum("p_den", [128, 1024])
                acc_rr = RR([psum("p_acc0", [128, 512])])
                tr_rr = RR([psum("p_tr0", [128, 512])])
            else:
                p_ot, B_pot = psum("p_ot", [128, 512])
                p_den, B_pden = psum("p_den", [128, 512])
                acc_rr = RR([psum("p_acc0", [128, 512]), psum("p_acc1", [128, 512])])
                tr_rr = RR([psum("p_tr0", [128, 512]), psum("p_tr1", [128, 512])])
            WQC = 1280 if phaseA else 1024
            WZC = 1024 if phaseA else 512
            wq = sb(ph, "wq", [128, 16, WQC], BF16); B_wq = Buf()
            wz = sb(ph, "wz", [128, 16, WZC], BF16); B_wz = Buf()
            bias_bc = sb(ph, "bias_bc", [128, WQC], F32); B_bias = Buf()
            cs_sb = sb(ph, "cs_sb", [128, 33, 16], F32); B_cs = Buf()
            kT_all = sb(ph, "kT_all", [128, 18, 2, 128], BF16); B_kT = [Buf() for _ in range(18)]
            v_all = sb(ph, "v_all", [128, 18, 256], BF16); B_v = [Buf() for _ in range(18)]
            xT = [sb(ph, f"xT{i}", [128, 16, 128], BF16) for i in range(2)]; B_xT = [Buf(), Buf()]
            qkv = [sb(ph, f"qkv{i}", [128, WQC], F32) for i in range(3)]; B_qkv = [Buf(), Buf(), Buf()]
            rt = [sb(ph, f"rt{i}", [128, 18, 8], F32) for i in range(4)]; B_rts = [Buf() for _ in range(4)]
            qb = sb(ph, "qb", [128, 1024], BF16); B_qb = Buf()
            kb = sb(ph, "kb", [128, 256], BF16); B_kb = Buf()
            qT = [sb(ph, f"qT{i}", [128, 8, 128], BF16) for i in range(2)]; B_qT = [Buf(), Buf()]
            PTs = [sb(ph, f"PT{i}", [128, 1024], BF16) for i in range(2)]; B_PTs = [Buf(), Buf()]
            pt_rr = [0]
            dsum = sb(ph, "dsum", [128, 1024 if phaseA else 512], F32); B_dsum = Buf()
            nsum = sb(ph, "nsum", [128, 1024 if phaseA else 512], F32); B_nsum = Buf()
            zTs = [sb(ph, f"zT{i}", [128, 8 if phaseA else 4, 128], BF16) for i in range(3)]; B_zTs = [Buf(), Buf(), Buf()]
            ytiles = [sb(ph, f"ytile{i}", [128, 8, 128], BF16) for i in range(2)]; B_yts = [Buf(), Buf()]
            cbks = [sb(ph, f"cbk{i}", [128, 8, 512], BF16) for i in range(2)]; B_cbks = [Buf(), Buf()]
            kTc = sb(ph, "kTc", [128, 8, 2, 128], BF16); B_kTc = Buf()
            if phaseA:
                sinkexp = sb(ph, "sinkexp", [128, 8], F32); B_sink = Buf()
                S.dma(lambda e: e.dma_start(out=sinkexp[:], in_=sink_d), writes=[B_sink])
                S.op("act", lambda e: e.activation(out=sinkexp[:], in_=sinkexp[:], func=AF.Exp), reads=[B_sink], writes=[B_sink])
            else:
                accN = sb(ph, "accN", [128, 4, NTOK], BF16); B_accN = Buf()
                accD = sb(ph, "accD", [128, 4, NTOK], BF16); B_accD = Buf()

            for gi, g in enumerate(groups):
                nm, col0, nq, nk, Hkv, G, dil, C = (g[k] for k in ("name", "col0", "nq", "nk", "Hkv", "G", "dil", "C"))
                isA = nm == "A"
                pidx = [x["name"] for x in GROUPS].index(nm)
                ncol = nq + 2 * nk
                NCH = Hkv // 2
                NQC = NCH * G
                OW = NQC * 128
                tiles = tile_list(g["order"])
                if DBG["tiles"] is not None:
                    tiles = tiles[:DBG["tiles"]]
                if not DBG["samp"]:
                    tiles = [t for t in tiles if t["kind"] != "samp"]
                zc0 = ZA0 if isA else ZB0
                need_z = isA or nm == "B0"

                S.dma(lambda e: e.dma_start(out=wq[:, :, 0:ncol],
                                            in_=w_in[:, col0:col0 + ncol].rearrange("(k p) n -> p k n", p=128)),
                      writes=[B_wq], eng="pool")
                S.dma(lambda e: e.dma_start(out=bias_bc[:, 0:ncol], in_=b_in[0, col0:col0 + ncol].partition_broadcast(128)),
                      writes=[B_bias])
                S.dma(lambda e: e.dma_start(out=cs_sb[:], in_=cs_d[pidx]), writes=[B_cs])
                if need_z:
                    for cq in range(NQC):
                        cp, gg = divmod(cq, G)
                        for e2 in range(2):
                            h = 2 * cp + e2
                            c_src = zc0 + h * G * 64 + gg * 64
                            S.dma(lambda e: e.dma_start(
                                out=wz[:, :, cq * 128 + e2 * 64: cq * 128 + e2 * 64 + 64],
                                in_=w_in[:, c_src:c_src + 64].rearrange("(k p) n -> p k n", p=128)),
                                writes=[B_wz], eng="pool")

                def stage0(ti, t):
                    issue_bg(1)
                    if phaseA:
                        issue_wconv(4)
                    yield from x_transpose(ti % 3, xT[ti % 2], B_xT[ti % 2], tr_rr)

                def stage1a(ti, t):
                    full = t["kind"] != "halo"
                    QKV, BQ = qkv[ti % 3], B_qkv[ti % 3]
                    XT, BXT = xT[ti % 2], B_xT[ti % 2]
                    c = 0 if full else nq
                    while c < ncol:
                        a0, a1 = c, min(c + 512, ncol)
                        c = a1
                        pacc, Bpa = acc_rr.next()
                        for k in range(16):
                            S.op("pe", lambda e: e.matmul(pacc[:, 0:a1 - a0], lhsT=XT[:, k, :], rhs=wq[:, k, a0:a1],
                                                          start=(k == 0), stop=(k == 15)), reads=[BXT, B_wq], writes=[Bpa])
                        S.op("dve", lambda e: e.tensor_tensor(out=QKV[:, a0:a1], in0=pacc[:, 0:a1 - a0], in1=bias_bc[:, a0:a1], op=ALU.add),
                             reads=[Bpa, B_bias], writes=[BQ])
                        yield

                def stage1z(ti, t):
                    full = t["kind"] != "halo"
                    XT, BXT = xT[ti % 2], B_xT[ti % 2]
                    if full and need_z:
                        ZT, BZT = zTs[ti % 3], B_zTs[ti % 3]
                        for c4 in range(0, NQC, 4):
                            pacc, Bpa = acc_rr.next()
                            for cq in range(c4, c4 + 4):
                                oc = (cq - c4) * 128
                                for k in range(16):
                                    S.op("pe", lambda e: e.matmul(pacc[:, oc:oc + 128], lhsT=wz[:, k, cq * 128:(cq + 1) * 128], rhs=XT[:, k, :],
                                                                  start=(k == 0), stop=False), reads=[B_wz, BXT], writes=[Bpa])
                                bo = (0 if isA else 1024) + cq * 128
                                S.op("pe", lambda e: e.matmul(pacc[:, oc:oc + 128], lhsT=bzr[:, bo:bo + 128], rhs=ones[:, 0:128],
                                                              start=False, stop=True), reads=[B_bzr, B_ones], writes=[Bpa])
                            S.op("act", lambda e: e.activation(out=ZT[:, c4:c4 + 4, :], in_=pacc[:].rearrange("p (a b) -> p a b", a=4),
                                                               func=AF.Silu), reads=[Bpa], writes=[BZT])
                            yield

                def stage1b_early(ti, t):
                    kind = t["kind"]
                    slot = t["slot"]
                    full = kind != "halo"
                    QKV, BQ = qkv[ti % 3], B_qkv[ti % 3]
                    r0 = 0 if full else nq
                    nh = (nq + nk - r0) // 64
                    X = QKV[:, r0:nq + nk].rearrange("p (h d) -> p h d", d=64)
                    x1, x2 = X[:, :, 0:8], X[:, :, 8:16]
                    cosb = cs_sb[:, ti, 0:8].unsqueeze(1).to_broadcast([128, nh, 8])
                    sinb = cs_sb[:, ti, 8:16].unsqueeze(1).to_broadcast([128, nh, 8])
                    tt = [r[:, 0:nh, :] for r in rt]
                    for i_, (i0, i1) in enumerate(((x1, cosb), (x2, sinb), (x2, cosb), (x1, sinb))):
                        S.op("pool", lambda e: e.tensor_tensor(out=tt[i_], in0=i0, in1=i1, op=ALU.mult), reads=[BQ, B_cs], writes=[B_rts[i_]])
                    S.op("pool", lambda e: e.tensor_tensor(out=x1, in0=tt[0], in1=tt[1], op=ALU.subtract), reads=[B_rts[0], B_rts[1]], writes=[BQ])
                    S.op("pool", lambda e: e.tensor_tensor(out=x2, in0=tt[2], in1=tt[3], op=ALU.add), reads=[B_rts[2], B_rts[3]], writes=[BQ])
                    kv_src = QKV[:, nq:nq + 2 * nk]
                    if kind == "main":
                        pr = g["pst_rows"]
                        if t["col0"] >= CH - pr:
                            ro = t["col0"] - (CH - pr)
                            S.dma(lambda e: e.dma_start(out=pst[nm][sl(ro, 128, t["cstep"]), :], in_=kv_src), reads=[BQ])
                    elif kind == "samp":
                        for s in range(NSAMP):
                            S.dma(lambda e: e.dma_start(out=sst[nm][s, C - 8:C, :], in_=QKV[s * 8:(s + 1) * 8, nq:nq + 2 * nk]),
                                  reads=[BQ])
                    S.op("pool", lambda e: e.tensor_copy(out=kb[:, 0:nk], in_=QKV[:, nq:nq + nk]), reads=[BQ], writes=[B_kb])
                    S.op("pool", lambda e: e.tensor_copy(out=v_all[:, slot, 0:nk], in_=QKV[:, nq + nk:nq + 2 * nk]),
                         reads=[BQ], writes=[B_v[slot]])
                    if full:
                        for cp in range(NCH):
                            src = QKV[:, cp * 2 * G * 64:(cp + 1) * 2 * G * 64].rearrange("p (e g d) -> p g e d", e=2, g=G)
                            dst = qb[:, cp * G * 128:(cp + 1) * G * 128].rearrange("p (g e d) -> p g e d", g=G, e=2)
                            S.op("pool", lambda e: e.tensor_copy(out=dst, in_=src), reads=[BQ], writes=[B_qb])

                def stage1b_late(ti, t):
                    slot = t["slot"]
                    full = t["kind"] != "halo"
                    for _ in pe_transposes([kb[:, cp * 128:(cp + 1) * 128] for cp in range(NCH)], [B_kb], tr_rr,
                                           lambda j0, n: kT_all[:, slot, j0:j0 + n, :], B_kT[slot]):
                        pass
                    if not full:
                        return
                    for _ in pe_transposes([qb[:, cq * 128:(cq + 1) * 128] for cq in range(NQC)], [B_qb], tr_rr,
                                           lambda j0, n: qT[ti % 2][:, j0:j0 + n, :], B_qT[ti % 2]):
                        pass

                def stage2(ti, t, par):
                    kind = t["kind"]
                    slot = t["slot"]
                    QT = qT[par]
                    BQT = B_qT[par]
                    started = set()

                    def pv(rhs_ap, v_ap, e2, out_fn, bank, reads, BPT):
                        for (ptile, Bp, lhs, rl) in ((p_ot, B_pot, v_ap, reads), (p_den, B_pden, ones[:, 0:64], [B_ones, BPT])):
                            key = (id(ptile), bank, e2)
                            first = key not in started
                            started.add(key)
                            S.op("pe", lambda e: e.matmul(out_fn(ptile), lhsT=lhs, rhs=rhs_ap, start=first, stop=True,
                                                          skip_group_check=True), reads=rl, writes=[Bp])

                    def qk_part(kslot, mask_off, hf):
                        kts = kT_all[:, kslot]
                        PT = PTs[pt_rr[0] % 2]
                        BPT = B_PTs[pt_rr[0] % 2]
                        pt_rr[0] += 1
                        if isA:
                            for e2 in range(2):
                                S.op("pe", lambda e: e.matmul(
                                    p_st[:, e2 * 512:(e2 + 1) * 512], lhsT=kts[64 * e2:64 * e2 + 64, 0, :],
                                    rhs=QT[64 * e2:64 * e2 + 64, hf * 4:(hf + 1) * 4, :], start=True, stop=True),
                                    reads=[B_kT[kslot], BQT], writes=[B_pst])
                        else:
                            for h in range(4):
                                cp, e2 = divmod(h, 2)
                                S.op("pe", lambda e: e.matmul(
                                    p_st[:, e2 * 512 + cp * 256:e2 * 512 + cp * 256 + 256], lhsT=kts[64 * e2:64 * e2 + 64, cp, :],
                                    rhs=QT[64 * e2:64 * e2 + 64, cp * 2:cp * 2 + 2, :], start=True, stop=True),
                                    reads=[B_kT[kslot], BQT], writes=[B_pst])
                        S.op("act", lambda e: e.activation(out=PT[:], in_=p_st[:], func=AF.Exp, scale=SCALE), reads=[B_pst], writes=[BPT])
                        PT3 = PT[:].rearrange("p (a b) -> p a b", a=8)
                        S.op("dve", lambda e: e.tensor_tensor(
                            out=PT3, in0=PT3, in1=maskt[:, mask_off:mask_off + 128].unsqueeze(1).to_broadcast([128, 8, 128]),
                            op=ALU.mult), reads=[BPT, B_mask], writes=[BPT])
                        return PT, BPT

                    def pv_part(kslot, hf, PT, BPT):
                        if isA:
                            for e2 in range(2):
                                pv(PT[:, e2 * 512:(e2 + 1) * 512], v_all[:, kslot, 64 * e2:64 * e2 + 64], e2,
                                   (lambda p, e2=e2, hf=hf: p[64 * e2:64 * e2 + 64, hf * 512:(hf + 1) * 512]), hf,
                                   [B_v[kslot], BPT], BPT)
                        else:
                            for h in range(4):
                                cp, e2 = divmod(h, 2)
                                pv(PT[:, e2 * 512 + cp * 256:e2 * 512 + cp * 256 + 256], v_all[:, kslot, 64 * h:64 * h + 64], e2,
                                   (lambda p, cp=cp, e2=e2: p[64 * e2:64 * e2 + 64, cp * 256:(cp + 1) * 256]), 0,
                                   [B_v[kslot], BPT], BPT)

                    def blocks_attn(blist):
                        subs = [(ks, mo, hf) for (ks, mo) in blist for hf in range(2 if isA else 1)]
                        prev_ = None
                        for sub in subs + [None]:
                            cur_ = None
                            if sub is not None:
                                PT, BPT = qk_part(*sub)
                                cur_ = (sub[0], sub[2], PT, BPT)
                                yield
                            if prev_ is not None:
                                pv_part(*prev_)
                                yield
                            prev_ = cur_

                    if kind == "main":
                        if isA:
                            moff = M_PA_H if t["prev_halo"] else M_PA
                        else:
                            moff = M_PB_H if t["prev_halo"] else M_PB
                        yield from blocks_attn([(slot, M_CUR), (t["prev"], moff)])
                    else:
                        yield from blocks_attn([(slot, {1: M_S1, 4: M_S4, 16: M_S16}[dil])])
                        nblk = {1: 1, 4: 4, 16: 8}[dil]
                        nt = {1: 8, 4: 2, 16: 1}[dil]
                        W = G * nt
                        half = nblk * NCH * W
                        def cache_load(s):
                            cbk, B_cbk = cbks[s % 2], B_cbks[s % 2]
                            if dil == 1:
                                S.dma(lambda e: e.dma_start(out=cbk[:, 0, 0:2 * nk], in_=cache[nm][s]), writes=[B_cbk], eng="pool")
                            elif dil == 4:
                                S.dma(lambda e: e.dma_start(out=cbk[:, 0:4, :], in_=cache[nm][s].rearrange("(m r) c -> m r c", r=4)),
                                      writes=[B_cbk], eng="pool")
                            else:
                                S.dma(lambda e: e.dma_start(out=cbk[:, 0:8, :],
                                                            in_=cache[nm][s].rearrange("(m r) c -> m r c", r=16)[:, 0:8, :]),
                                      writes=[B_cbk], eng="pool")
                        cache_load(0)
                        for s in range(NSAMP):
                            cbk = cbks[s % 2]
                            B_cbk = B_cbks[s % 2]
                            if s + 1 < NSAMP:
                                cache_load(s + 1)
                            items = [(b_, cp) for b_ in range(nblk) for cp in range(NCH)]
                            kTc_flat = kTc[:].rearrange("p a c b -> p (a c) b")
                            for _ in pe_transposes([cbk[:, b_, cp * 128:(cp + 1) * 128] for (b_, cp) in items], [B_cbk], tr_rr,
                                                   lambda j0, n: kTc_flat[:, j0:j0 + n, :], B_kTc):
                                pass

                            def tokc(b_):
                                if dil == 1:
                                    return slice(s * 8, s * 8 + 8)
                                if dil == 4:
                                    return slice(s * 8 + b_, s * 8 + b_ + 5, 4)
                                return slice(s * 8 + b_, s * 8 + b_ + 1)
                            PT = PTs[pt_rr[0] % 2]
                            BPT = B_PTs[pt_rr[0] % 2]
                            pt_rr[0] += 1
                            for b_ in range(nblk):
                                for h in range(Hkv):
                                    cp, e2 = divmod(h, 2)
                                    o0 = e2 * 512 + (b_ * NCH + cp) * W
                                    tk = tokc(b_)
                                    S.op("pe", lambda e: e.matmul(
                                        p_st[:, o0:o0 + W].rearrange("p (g t) -> p g t", g=G), lhsT=kTc_flat[64 * e2:64 * e2 + 64, b_ * NCH + cp, :],
                                        rhs=QT[64 * e2:64 * e2 + 64, cp * G:(cp + 1) * G, tk], start=True, stop=True),
                                        reads=[B_kTc, BQT], writes=[B_pst])
                            PTh = PT[:].rearrange("p (e c) -> p e c", e=2)[:, :, 0:half]
                            S.op("act", lambda e: e.activation(out=PTh, in_=p_st[:].rearrange("p (e c) -> p e c", e=2)[:, :, 0:half],
                                                               func=AF.Exp, scale=SCALE), reads=[B_pst], writes=[BPT])
                            if dil != 16:
                                moff = M_CA if isA else (M_CB0 if dil == 1 else M_CB1)
                                PTm = PTh.rearrange("p e (a t) -> p e a t", t=nt)
                                S.op("dve", lambda e: e.tensor_tensor(
                                    out=PTm, in0=PTm,
                                    in1=maskt[:, moff:moff + nt].unsqueeze(1).unsqueeze(1).to_broadcast([128, 2, half // nt, nt]),
                                    op=ALU.mult), reads=[BPT, B_mask], writes=[BPT])
                            yield
                            for b_ in range(nblk):
                                for h in range(Hkv):
                                    cp, e2 = divmod(h, 2)
                                    o0 = e2 * 512 + (b_ * NCH + cp) * W
                                    tk = tokc(b_)
                                    gs = 4 if isA else G
                                    for g0 in range(0, G, gs):
                                        c_lo_ = (cp * G + g0) * 128
                                        pv(PT[:, o0 + g0 * nt:o0 + (g0 + gs) * nt].rearrange("p (g t) -> p g t", g=gs),
                                           cbk[:, b_, nk + 64 * h:nk + 64 * h + 64], e2,
                                           (lambda p, e2=e2, tk=tk, c_lo_=c_lo_, gs=gs: p[64 * e2:64 * e2 + 64, c_lo_:c_lo_ + gs * 128]
                                            .rearrange("p (g q) -> p g q", g=gs)[:, :, tk]), c_lo_ // 512,
                                           [B_cbk, BPT], BPT)
                            yield

                    cols = sl(t["col0"], 128, t["cstep"])
                    if nm in ("B2", "B1"):
                        aN = accN[:, :, cols]
                        aD = accD[:, :, cols]
                        o3 = p_ot[:, 0:512].rearrange("p (a b) -> p a b", a=4)
                        d3 = p_den[:, 0:512].rearrange("p (a b) -> p a b", a=4)
                        if nm == "B2":
                            S.op("act", lambda e: e.activation(out=aN, in_=o3, func=AF.Copy), reads=[B_pot], writes=[B_accN])
                            S.op("dve", lambda e: e.tensor_copy(out=aD, in_=d3), reads=[B_pden], writes=[B_accD])
                        else:
                            S.op("dve", lambda e: e.tensor_tensor(out=aN, in0=o3, in1=aN, op=ALU.add), reads=[B_pot, B_accN], writes=[B_accN])
                            S.op("dve", lambda e: e.tensor_tensor(out=aD, in0=d3, in1=aD, op=ALU.add), reads=[B_pden, B_accD], writes=[B_accD])
                        return
                    n3 = nsum[:, 0:OW].rearrange("p (a b) -> p a b", a=NQC)
                    d3s = dsum[:, 0:OW].rearrange("p (a b) -> p a b", a=NQC)
                    o3 = p_ot[:, 0:OW].rearrange("p (a b) -> p a b", a=NQC)
                    d3 = p_den[:, 0:OW].rearrange("p (a b) -> p a b", a=NQC)
                    if isA:
                        S.op("dve", lambda e: e.tensor_tensor(out=d3s, in0=d3, in1=sinkexp[:].unsqueeze(2).to_broadcast([128, 8, 128]),
                                                              op=ALU.add), reads=[B_pden, B_sink], writes=[B_dsum])
                        S.op("act", lambda e: e.activation(out=nsum[:, 0:OW], in_=p_ot[:, 0:OW], func=AF.Copy), reads=[B_pot], writes=[B_nsum])
                    else:
                        S.op("dve", lambda e: e.tensor_tensor(out=d3s, in0=d3, in1=accD[:, :, cols], op=ALU.add),
                             reads=[B_pden, B_accD], writes=[B_dsum])
                        S.op("dve", lambda e: e.tensor_tensor(out=n3, in0=o3, in1=accN[:, :, cols], op=ALU.add),
                             reads=[B_pot, B_accN], writes=[B_nsum])
                    S.op("act", lambda e: e.activation(out=dsum[:, 0:OW], in_=dsum[:, 0:OW], func=AF.Ln), reads=[B_dsum], writes=[B_dsum])
                    S.op("act", lambda e: e.activation(out=dsum[:, 0:OW], in_=dsum[:, 0:OW], func=AF.Exp, scale=-1.0),
                         reads=[B_dsum], writes=[B_dsum])
                    S.op("dve", lambda e: e.tensor_tensor(out=nsum[:, 0:OW], in0=nsum[:, 0:OW], in1=dsum[:, 0:OW], op=ALU.mult),
                         reads=[B_nsum, B_dsum], writes=[B_nsum])
                    ytile, B_yt = ytiles[ti % 2], B_yts[ti % 2]
                    S.op("dve", lambda e: e.tensor_tensor(out=ytile[:, 0:NQC, :], in0=n3, in1=zTs[ti % 3][:, 0:NQC, :], op=ALU.mult),
                         reads=[B_nsum, B_zTs[ti % 3]], writes=[B_yt])
                    ydst = ya_s if isA else yb_s
                    S.dma(lambda e: e.dma_start(out=ydst[:, :, cols], in_=ytile[:, 0:NQC, :]), reads=[B_yt])
                    yield

                nT = len(tiles)
                x_load(tiles[0]["row0"], tiles[0]["step"], 0)
                if nT > 1:
                    x_load(tiles[1]["row0"], tiles[1]["step"], 1)
                for r in range(nT + 3):
                    if r + 2 < nT:
                        x_load(tiles[r + 2]["row0"], tiles[r + 2]["step"], (r + 2) % 3)
                    t0_, t1a, t1b, t2 = r, r - 1, r - 2, r - 3
                    g2 = None
                    if 0 <= t2 < nT and tiles[t2]["kind"] != "halo" and DBG["attn"]:
                        g2 = stage2(t2, tiles[t2], t2 % 2)
                    g1a = stage1a(t1a, tiles[t1a]) if 0 <= t1a < nT else None
                    if 0 <= t1b < nT:
                        stage1b_early(t1b, tiles[t1b])
                    if g2 is not None:
                        try:
                            next(g2)
                        except StopIteration:
                            g2 = None
                    g0 = stage0(t0_, tiles[t0_]) if 0 <= t0_ < nT else None
                    run_interleaved([g1a, g2, g0])
                    if 0 <= t1a < nT:
                        for _ in stage1z(t1a, tiles[t1a]):
                            pass
                    if 0 <= t1b < nT:
                        stage1b_late(t1b, tiles[t1b])

        bgroups = [g for g in GROUPS if g["name"] != "A" and (DBG["groups"] is None or g["name"] in DBG["groups"])]
        agroups = [g for g in GROUPS if g["name"] == "A" and (DBG["groups"] is None or g["name"] in DBG["groups"])]
        with ExitStack() as ph:
            attention_phase(bgroups, ph, False)
        S.barrier()
        with ExitStack() as ph:
            attention_phase(agroups, ph, True)
        issue_bg(len(bg_pending))
        issue_wconv(len(wconv_pending))
        S.barrier()

        with ExitStack() as ph:
            def psum(name, shp, dt=F32):
                uniq[0] += 1
                return ph.enter_context(nc.psum_tensor(f"{name}_{uniq[0]}", shp, dt)), Buf()
            p_ga, B_pga = psum("p_ga", [128, 512])
            p_gb, B_pgb = psum("p_gb", [128, 512])
            p_ma, B_pma = psum("p_ma", [128, 512])
            p_mb, B_pmb = psum("p_mb", [128, 512])
            acc_rr = RR([psum("p_acc0", [128, 512]), psum("p_acc1", [128, 512])])
            tr_rr = RR([psum("p_tr0", [128, 512]), psum("p_tr1", [128, 512])])
            NH = 768
            mTh = sb(ph, "mTh", [128, 16, NH], BF16); B_mTh = Buf()
            lng = sb(ph, "lng", [128, DM], F32); B_lng = Buf()
            lnb = sb(ph, "lnb", [128, DM], F32); B_lnb = Buf()
            S.dma(lambda e: e.dma_start(out=lng[:], in_=ln_g[0, :].partition_broadcast(128)), writes=[B_lng])
            S.dma(lambda e: e.dma_start(out=lnb[:], in_=ln_b[0, :].partition_broadcast(128)), writes=[B_lnb])

            halves = [(0, 6), (6, 12), (12, 17)] if DBG["out"] else []
            for hi, (t0, t1) in enumerate(halves):
                nt_h = t1 - t0
                ntok = 128 * nt_h
                tok0 = 128 * t0
                with ExitStack() as s1:
                    xTh = sb(s1, "xTh", [128, 16, NH], BF16); B_xTh = Buf()
                    yaq = sb(s1, "yaq", [128, 8, NH], BF16); B_yaq = Buf()
                    ybq = sb(s1, "ybq", [128, 4, NH], BF16); B_ybq = Buf()
                    wch = [sb(s1, f"wch{i}", [128, WCH], BF16) for i in range(2)]; B_wch = [Buf(), Buf()]
                    sa = [sb(s1, f"sa{i}", [128, 512], F32) for i in range(2)]; B_sa = [Buf(), Buf()]
                    sbg = [sb(s1, f"sbg{i}", [128, 512], F32) for i in range(2)]; B_sbg = [Buf(), Buf()]
                    tA = [sb(s1, f"tA{i}", [128, 512], F32) for i in range(2)]; B_tA = [Buf(), Buf()]
                    tB = [sb(s1, f"tB{i}", [128, 512], F32) for i in range(2)]; B_tB = [Buf(), Buf()]
                    S.dma(lambda e: e.dma_start(out=yaq[:, :, 0:ntok], in_=ya_s[:, :, tok0:tok0 + ntok]), writes=[B_yaq])
                    S.dma(lambda e: e.dma_start(out=ybq[:, :, 0:ntok], in_=yb_s[:, :, tok0:tok0 + ntok]), writes=[B_ybq])
                    for tl in range(min(2, nt_h)):
                        x_load(2048 + 128 * (t0 + tl), 1, tl % 3)
                    for tl in range(nt_h):
                        if tl + 2 < nt_h:
                            x_load(2048 + 128 * (t0 + tl + 2), 1, (tl + 2) % 3)
                        for _ in x_transpose(tl % 3, xTh[:, :, tl * 128:(tl + 1) * 128], B_xTh, tr_rr):
                            pass
                    tgs = [(n0, min(n0 + 512, ntok)) for n0 in range(0, ntok, 512)]
                    it = 0
                    def wviews(cg2):
                        w_ = wch[cg2 % 2]
                        return (w_, B_wch[cg2 % 2],
                                w_[:, 0:4096].rearrange("p (k n) -> p k n", k=16),
                                w_[:, 4096:8192].rearrange("p (k n) -> p k n", k=16),
                                w_[:, 8192:10240].rearrange("p (k n) -> p k n", k=8),
                                w_[:, 10240:11264].rearrange("p (k n) -> p k n", k=4))

                    def wload(cg2):
                        w_, Bw, wga2, wgb2, wba2, wbb2 = wviews(cg2)
                        S.dma(lambda e: e.dma_start(out=w_[:], in_=wsc[cg2]), reads=[B_wsc[cg2]], writes=[Bw])

                    wload(0)
                    for c in range(16):
                        cg2, ci = divmod(c, 2)
                        w_, Bw, wga2, wgb2, wba2, wbb2 = wviews(cg2)
                        wga = wga2[:, :, ci * 128:(ci + 1) * 128]
                        wgb = wgb2[:, :, ci * 128:(ci + 1) * 128]
                        wba = wba2[:, :, ci * 128:(ci + 1) * 128]
                        wbb = wbb2[:, :, ci * 128:(ci + 1) * 128]
                        if ci == 0 and cg2 + 1 < 8:
                            wload(cg2 + 1)
                        for (n0, n1) in tgs:
                            n = n1 - n0
                            j = it % 2
                            it += 1
                            for k in range(16):
                                S.op("pe", lambda e: e.matmul(p_ga[:, 0:n], lhsT=wga[:, k, :], rhs=xTh[:, k, n0:n1],
                                                              start=(k == 0), stop=(k == 15)), reads=[Bw, B_xTh], writes=[B_pga])
                            S.op("act", lambda e: e.activation(out=sa[j][:, 0:n], in_=p_ga[:, 0:n], func=AF.Sigmoid,
                                                               bias=bfm[:, 12 + c:13 + c]), reads=[B_pga, B_bfm], writes=[B_sa[j]])
                            for k in range(16):
                                S.op("pe", lambda e: e.matmul(p_gb[:, 0:n], lhsT=wgb[:, k, :], rhs=xTh[:, k, n0:n1],
                                                              start=(k == 0), stop=(k == 15)), reads=[Bw, B_xTh], writes=[B_pgb])
                            S.op("act", lambda e: e.activation(out=sbg[j][:, 0:n], in_=p_gb[:, 0:n], func=AF.Sigmoid,
                                                               bias=bfm[:, 28 + c:29 + c]), reads=[B_pgb, B_bfm], writes=[B_sbg[j]])
                            for k in range(8):
                                S.op("pe", lambda e: e.matmul(p_ma[:, 0:n], lhsT=wba[:, k, :], rhs=yaq[:, k, n0:n1],
                                                              start=(k == 0), stop=(k == 7)), reads=[Bw, B_yaq], writes=[B_pma])
                            for k in range(4):
                                S.op("pe", lambda e: e.matmul(p_mb[:, 0:n], lhsT=wbb[:, k, :], rhs=ybq[:, k, n0:n1],
                                                              start=(k == 0), stop=(k == 3)), reads=[Bw, B_ybq], writes=[B_pmb])
                            S.op("dve", lambda e: e.tensor_tensor(out=tA[j][:, 0:n], in0=p_ma[:, 0:n], in1=sa[j][:, 0:n], op=ALU.mult),
                                 reads=[B_pma, B_sa[j]], writes=[B_tA[j]])
                            S.op("dve", lambda e: e.tensor_tensor(out=tB[j][:, 0:n], in0=p_mb[:, 0:n], in1=sbg[j][:, 0:n], op=ALU.mult),
                                 reads=[B_pmb, B_sbg[j]], writes=[B_tB[j]])
                            S.op("pool", lambda e: e.tensor_tensor(out=mTh[:, c, n0:n1], in0=tA[j][:, 0:n], in1=tB[j][:, 0:n], op=ALU.add),
                                 reads=[B_tA[j], B_tB[j]], writes=[B_mTh])
                S.barrier()
                with ExitStack() as s2:
                    wo = sb(s2, "wo", [128, 16, DM], BF16); B_wo = [Buf() for _ in range(4)]
                    hs = [sb(s2, f"hs{i}", [128, DM], F32) for i in range(2)]; B_hs = [Buf(), Buf()]
                    hn = [sb(s2, f"hn{i}", [128, DM], F32) for i in range(2)]; B_hn = [Buf(), Buf()]
                    stats = sb(s2, "stats", [128, 4, 6], F32); B_stats = Buf()
                    mv = sb(s2, "mv", [128, 4], F32); B_mv = Buf()
                    for cg in range(4):
                        S.dma(lambda e: e.dma_start(out=wo[:, :, cg * 512:(cg + 1) * 512], in_=wosc[:, :, cg * 512:(cg + 1) * 512]),
                              reads=[B_wosc[cg]], writes=[B_wo[cg]])
                    mo = [sb(s2, f"mo{i}", [128, DM], F32) for i in range(2)]; B_mo = [Buf(), Buf()]

                    def F1(tl):
                        j = tl % 2
                        row0 = 2048 + 128 * (t0 + tl)
                        S.dma(lambda e: e.dma_start(out=hs[j][:], in_=xe[row0:row0 + 128, :]), writes=[B_hs[j]])
                        for cg in range(4):
                            pacc, Bpa = acc_rr.next()
                            for k in range(16):
                                S.op("pe", lambda e: e.matmul(pacc[:], lhsT=mTh[:, k, tl * 128:(tl + 1) * 128], rhs=wo[:, k, cg * 512:(cg + 1) * 512],
                                                              start=(k == 0), stop=(k == 15)), reads=[B_mTh, B_wo[cg]], writes=[Bpa])
                            S.op("act", lambda e: e.activation(out=mo[j][:, cg * 512:(cg + 1) * 512], in_=pacc[:], func=AF.Copy),
                                 reads=[Bpa], writes=[B_mo[j]])

                    def F2(tl):
                        j = tl % 2
                        S.op("dve", lambda e: e.scalar_tensor_tensor(out=hs[j][:], in0=hs[j][:], scalar=ALPHA, in1=mo[j][:],
                                                                    op0=ALU.mult, op1=ALU.add), reads=[B_hs[j], B_mo[j]], writes=[B_hs[j]])
                        for i4 in range(4):
                            S.op("dve", lambda e: e.bn_stats(out=stats[:, i4, :], in_=hs[j][:, i4 * 512:(i4 + 1) * 512]),
                                 reads=[B_hs[j]], writes=[B_stats])
                        S.op("dve", lambda e: e.bn_aggr(out=mv[:, 0:2], in_=stats[:].rearrange("p a b -> p (a b)")), reads=[B_stats], writes=[B_mv])
                        S.op("act", lambda e: e.activation(out=mv[:, 2:3], in_=mv[:, 1:2], func=AF.Sqrt, bias=LN_EPS), reads=[B_mv], writes=[B_mv])
                        S.op("dve", lambda e: e.reciprocal(out=mv[:, 2:3], in_=mv[:, 2:3]), reads=[B_mv], writes=[B_mv])
                        S.op("dve", lambda e: e.tensor_tensor(out=mv[:, 3:4], in0=mv[:, 0:1], in1=mv[:, 2:3], op=ALU.mult), reads=[B_mv], writes=[B_mv])
                        S.op("dve", lambda e: e.tensor_scalar(out=mv[:, 3:4], in0=mv[:, 3:4], scalar1=-1.0, scalar2=None, op0=ALU.mult),
                             reads=[B_mv], writes=[B_mv])
                        S.op("pool", lambda e: e.tensor_scalar(out=hn[j][:], in0=hs[j][:], scalar1=mv[:, 2:3], scalar2=mv[:, 3:4],
                                                              op0=ALU.mult, op1=ALU.add), reads=[B_hs[j], B_mv], writes=[B_hn[j]])
                        S.op("pool", lambda e: e.tensor_tensor(out=hn[j][:], in0=hn[j][:], in1=lng[:], op=ALU.mult), reads=[B_hn[j], B_lng], writes=[B_hn[j]])
                        S.op("pool", lambda e: e.tensor_tensor(out=hn[j][:], in0=hn[j][:], in1=lnb[:], op=ALU.add), reads=[B_hn[j], B_lnb], writes=[B_hn[j]])
                        r0 = 128 * (t0 + tl)
                        S.dma(lambda e: e.dma_start(out=y_d[r0:r0 + 128, :], in_=hn[j][:]), reads=[B_hn[j]])

                    F1(0)
                    for tl in range(nt_h):
                        if tl + 1 < nt_h:
                            F1(tl + 1)
                        F2(tl)
                S.barrier()
        S.barrier()
        S.emit()
    return nc


def _masks(halo_valid):
    k = np.arange(128)[:, None]
    q = np.arange(128)[None, :]
    m = np.zeros((128, NMASK), np.float32)
    m[:, M_CUR:M_CUR + 128] = (k <= q)
    m[:, M_PA:M_PA + 128] = (k > q)
    m[:, M_PB:M_PB + 128] = (k >= q)
    m[:, M_PA_H:M_PA_H + 128] = (k > q) * halo_valid
    m[:, M_PB_H:M_PB_H + 128] = (k >= q) * halo_valid
    ss, ts = k // 8, k % 8
    sq, tq = q // 8, q % 8
    same = (ss == sq) & (ts <= tq)
    m[:, M_S1:M_S1 + 128] = same
    m[:, M_S4:M_S4 + 128] = same & ((tq - ts) % 4 == 0)
    m[:, M_S16:M_S16 + 128] = same & (tq == ts)
    t8 = np.arange(8)[None, :]
    m[:, M_CA:M_CA + 8] = (k >= t8 + 1)
    m[:, M_CB0:M_CB0 + 8] = (k >= t8)
    m[:, M_CB1] = 1.0
    m[:, M_CB1 + 1] = (k[:, 0] >= 1)
    m[:, M_ID:M_ID + 128] = np.eye(128)
    return m


def _rope_tables(start):
    inv = np.power(np.float32(THETA), -(np.arange(0, 16, 2, dtype=np.float32) / np.float32(16))).astype(np.float32)
    cs = np.zeros((4, 128, 33, 16), np.float32)
    p = np.arange(128)
    for gi, g in enumerate(GROUPS):
        for ti, t in enumerate(tile_list(g["order"])):
            if t["kind"] == "samp":
                pos = PAST + (p % 8)
            else:
                pos = (start - 2048) + t["row0"] + t["step"] * p
            ang = pos.astype(np.float32)[:, None] * inv[None, :]
            cs[gi, :, ti, 0:8] = np.cos(ang)
            cs[gi, :, ti, 8:16] = np.sin(ang)
    return cs


_NC_CACHE = {}


def kernel(x_prompt, x_sample, cache_a_kv, cache_b0_kv, cache_b1_kv, cache_b2_kv,
           w_in, b_in, sink_a, w_br_a, w_br_b, w_out, ln_g, ln_b):
    f32 = np.float32
    x_prompt = np.asarray(x_prompt, f32); x_sample = np.asarray(x_sample, f32)
    w_in2 = np.ascontiguousarray(np.asarray(w_in, f32)[0])
    b_in2 = np.ascontiguousarray(np.asarray(b_in, f32))
    bflat = b_in2[0]
    p = np.arange(128)
    e_, d_ = p // 64, p % 64
    bfm = np.zeros((128, 44), f32)
    for g in range(8):
        bfm[:, g] = bflat[ZA0 + e_ * 512 + g * 64 + d_]
    for cq in range(4):
        cp, gg = divmod(cq, 2)
        bfm[:, 8 + cq] = bflat[ZB0 + (2 * cp + e_) * 128 + gg * 64 + d_]
    for c in range(16):
        bfm[:, 12 + c] = bflat[GA0 + 128 * c + p]
        bfm[:, 28 + c] = bflat[GB0 + 128 * c + p]
    sk = np.asarray(sink_a, f32)[0]
    sinkl = np.zeros((128, 8), f32)
    for g in range(8):
        sinkl[:, g] = sk[e_ * 8 + g]
    caches = {"cache_a": np.asarray(cache_a_kv, f32)[0].reshape(128, 128, 256),
              "cache_b0": np.asarray(cache_b0_kv, f32)[0].reshape(128, 128, 512),
              "cache_b1": np.asarray(cache_b1_kv, f32)[0].reshape(128, 512, 512),
              "cache_b2": np.asarray(cache_b2_kv, f32)[0].reshape(128, 2048, 512)}
    bzrow = np.ascontiguousarray(bfm[:, 0:12].T.reshape(1, 1536))
    common = {"w_in": w_in2, "b_in": b_in2, "bfm": bfm, "bzrow": bzrow, "sinkl": sinkl,
              "w_br_a": np.ascontiguousarray(np.asarray(w_br_a, f32)[0]),
              "w_br_b": np.ascontiguousarray(np.asarray(w_br_b, f32)[0]),
              "w_out": np.ascontiguousarray(np.asarray(w_out, f32)[0]),
              "ln_g": np.ascontiguousarray(np.asarray(ln_g, f32)), "ln_b": np.ascontiguousarray(np.asarray(ln_b, f32))}
    in_maps = []
    for c in range(NCORE):
        n, j = divmod(c, 4)
        start = CH * j
        xe = np.zeros((4224, DM), f32)
        if j > 0:
            xe[0:2048] = x_prompt[n, start - 2048:start]
        xe[2048:4096] = x_prompt[n, start:start + CH]
        xe[4096:] = x_sample[NSAMP * c:NSAMP * (c + 1)].reshape(128, DM)
        m = dict(common)
        m["xe"] = xe
        m["cs"] = _rope_tables(start)
        m["masks"] = _masks(1.0 if j > 0 else 0.0)
        for k, v in caches.items():
            m[k] = np.ascontiguousarray(v[NSAMP * c:NSAMP * (c + 1)])
        in_maps.append(m)

    if "nc" not in _NC_CACHE:
        _NC_CACHE["nc"] = build_program()
    res = run_bass_kernel_spmd(_NC_CACHE["nc"], in_maps, core_ids=list(range(NCORE)))
    R = res.results

    y_prompt = np.zeros((2, 8192, DM), f32)
    y_sample = np.zeros((128, 8, DM), f32)
    for c in range(NCORE):
        n, j = divmod(c, 4)
        y_prompt[n, CH * j:CH * (j + 1)] = R[c]["y"][0:CH]
        y_sample[NSAMP * c:NSAMP * (c + 1)] = R[c]["y"][CH:].reshape(NSAMP, 8, DM)
    shp = {"a": (2, 2, 64), "b0": (2, 4, 64), "b1": (2, 4, 64), "b2": (2, 4, 64)}
    rows = {"a": 128, "b0": 128, "b1": 512, "b2": 2048}
    pouts, souts = [], []
    for key in ("a", "b0", "b1", "b2"):
        po = np.stack([R[4 * n + 3]["pst_" + key].reshape((rows[key],) + shp[key]) for n in range(2)], 0)[None]
        pouts.append(np.ascontiguousarray(po.astype(f32)))
        so = np.concatenate([R[c]["sst_" + key].reshape((NSAMP, rows[key]) + shp[key]) for c in range(NCORE)], 0)[None]
        souts.append(np.ascontiguousarray(so.astype(f32)))
    return (y_prompt, y_sample, pouts[0], pouts[1], pouts[2], pouts[3], souts[0], souts[1], souts[2], souts[3])
```
